# Optimizing a Trainium2 kernel written in Bass

```python
import jax, jax.numpy as jnp
from jax import lax
import numpy as np

D_MODEL = 2048
BATCH = 2
SEQ = 4096
DEPTH = 1

MIX_WIDTH = D_MODEL
GLA_HEADS = 4
GLA_VALUE_DIM = MIX_WIDTH // 2
GLA_KEY_DIM = GLA_VALUE_DIM // 2
GLA_HEAD_K = GLA_KEY_DIM // GLA_HEADS
GLA_HEAD_V = GLA_VALUE_DIM // GLA_HEADS
GLA_GATE_RANK = 16
GLA_GATE_NORMALIZER = 16.0
HG_WIDTH = MIX_WIDTH - GLA_VALUE_DIM
HG_EXPAND = 128
HG_HEADS = HG_WIDTH // HG_EXPAND
HG_HEAD_V = HG_WIDTH // HG_HEADS
HG_FORGET_DIM = HG_HEADS * HG_EXPAND
IN_WIDTH = 2 * GLA_KEY_DIM + 2 * GLA_VALUE_DIM + GLA_GATE_RANK + 2 * HG_FORGET_DIM + 2 * HG_WIDTH
FFN_HIDDEN = -(-8 * D_MODEL // (3 * 256)) * 256
CHUNK = 64
NORM_EPS = 1e-6

kernel_name = "hymba_gla_hgrn2_adaln_block"


def rms_norm(x, w):
    xf = x.astype(jnp.float32)
    y = xf * lax.rsqrt(jnp.mean(xf * xf, axis=-1, keepdims=True) + NORM_EPS)
    return (y * w.astype(jnp.float32)).astype(x.dtype)


def chunk_gated_linear_attention(q, k, v, log_g, scale):
    B, T, H, Dk = q.shape
    Dv = v.shape[-1]
    N = T // CHUNK

    def to_chunks(a):
        return a.astype(jnp.float32).reshape(B, N, CHUNK, H, a.shape[-1]).transpose(1, 0, 3, 2, 4)

    qc, kc, vc, gc = to_chunks(q * scale), to_chunks(k), to_chunks(v), to_chunks(log_g)
    causal = jnp.tril(jnp.ones((CHUNK, CHUNK), dtype=bool))[:, :, None]

    def step(S, inp):
        qi, ki, vi, gi = inp
        b = jnp.cumsum(gi, axis=-2)
        b_last = b[..., -1:, :]
        o_inter = jnp.einsum('bhcd,bhde->bhce', qi * jnp.exp(b), S)
        diff = b[..., :, None, :] - b[..., None, :, :]
        decay = jnp.exp(jnp.where(causal, diff, -jnp.inf))
        scores = jnp.einsum('bhid,bhjd,bhijd->bhij', qi, ki, decay)
        o = o_inter + jnp.einsum('bhij,bhje->bhie', scores, vi)
        S_new = jnp.swapaxes(jnp.exp(b_last), -1, -2) * S + jnp.einsum(
            'bhcd,bhce->bhde', ki * jnp.exp(b_last - b), vi)
        return S_new, o

    S0 = jnp.zeros((B, H, Dk, Dv), jnp.float32)
    _, o = lax.scan(step, S0, (qc, kc, vc, gc))
    return o.transpose(1, 0, 3, 2, 4).reshape(B, T, H, Dv).astype(v.dtype)


def split_in_projection(p):
    sizes = (GLA_KEY_DIM, GLA_KEY_DIM, GLA_VALUE_DIM, GLA_VALUE_DIM, GLA_GATE_RANK,
             HG_FORGET_DIM, HG_FORGET_DIM, HG_WIDTH, HG_WIDTH)
    idx = np.cumsum(sizes)[:-1].tolist()
    return jnp.split(p, idx, axis=-1)


def hybrid_mixer(h, w_in, w_gla_gate, b_gla_gate, gla_norm_w, lb, hg_norm_w, w_out):
    B, T, _ = h.shape
    p = h @ w_in
    gla_q, gla_k, gla_v, gla_g, gla_lr, hg_q, hg_f, hg_i, hg_g = split_in_projection(p)

    log_alpha = jax.nn.log_sigmoid((gla_lr @ w_gla_gate + b_gla_gate).astype(jnp.float32)) / GLA_GATE_NORMALIZER
    o_gla = chunk_gated_linear_attention(
        gla_q.reshape(B, T, GLA_HEADS, GLA_HEAD_K),
        gla_k.reshape(B, T, GLA_HEADS, GLA_HEAD_K),
        gla_v.reshape(B, T, GLA_HEADS, GLA_HEAD_V),
        log_alpha.reshape(B, T, GLA_HEADS, GLA_HEAD_K),
        GLA_HEAD_K ** -0.5)
    o_gla = rms_norm(o_gla, gla_norm_w).reshape(B, T, GLA_VALUE_DIM) * jax.nn.silu(gla_g)

    f = lb + (1.0 - lb) * jax.nn.sigmoid(hg_f.astype(jnp.float32))
    o_hg = chunk_gated_linear_attention(
        jax.nn.silu(hg_q).reshape(B, T, HG_HEADS, HG_EXPAND),
        (1.0 - f).reshape(B, T, HG_HEADS, HG_EXPAND),
        hg_i.reshape(B, T, HG_HEADS, HG_HEAD_V),
        jnp.log(f).reshape(B, T, HG_HEADS, HG_EXPAND),
        1.0)
    o_hg = rms_norm(o_hg, hg_norm_w).reshape(B, T, HG_WIDTH) * jax.nn.silu(hg_g)

    return jnp.concatenate([o_gla, o_hg], axis=-1) @ w_out


def swiglu(h, w_ffn_in, w_ffn_out):
    a, u = jnp.split(h @ w_ffn_in, 2, axis=-1)
    return (jax.nn.silu(a) * u) @ w_ffn_out


def setup_inputs(seed: int = 0) -> dict:
    key = jax.random.key(seed)
    ks = jax.random.split(key, 18)

    def nrm(k, shape, s):
        return jax.random.normal(k, shape, jnp.float32) * s

    return {
        "x": nrm(ks[0], (BATCH, SEQ, D_MODEL), 1.0),
        "c": nrm(ks[1], (BATCH, D_MODEL), 1.0),
        "w_ada": nrm(ks[2], (DEPTH, D_MODEL, 6 * D_MODEL), 0.5 * D_MODEL ** -0.5),
        "b_ada": nrm(ks[3], (DEPTH, 6 * D_MODEL), 0.01),
        "norm_mix_w": 1.0 + nrm(ks[4], (DEPTH, D_MODEL), 0.02),
        "w_in": nrm(ks[5], (DEPTH, D_MODEL, IN_WIDTH), D_MODEL ** -0.5),
        "w_gla_gate": nrm(ks[6], (DEPTH, GLA_GATE_RANK, GLA_KEY_DIM), GLA_GATE_RANK ** -0.5),
        "b_gla_gate": nrm(ks[7], (DEPTH, GLA_KEY_DIM), 0.1),
        "gla_norm_w": 1.0 + nrm(ks[8], (DEPTH, GLA_HEAD_V), 0.02),
        "hg_lower_bound_logits": nrm(ks[9], (DEPTH + 1, HG_FORGET_DIM), 0.1),
        "hg_norm_w": 1.0 + nrm(ks[10], (DEPTH, HG_HEAD_V), 0.02),
        "w_out": nrm(ks[11], (DEPTH, MIX_WIDTH, D_MODEL), MIX_WIDTH ** -0.5),
        "norm_ffn_w": 1.0 + nrm(ks[12], (DEPTH, D_MODEL), 0.02),
        "w_ffn_in": nrm(ks[13], (DEPTH, D_MODEL, 2 * FFN_HIDDEN), D_MODEL ** -0.5),
        "w_ffn_out": nrm(ks[14], (DEPTH, FFN_HIDDEN, D_MODEL), FFN_HIDDEN ** -0.5),
        "final_norm_w": 1.0 + nrm(ks[15], (D_MODEL,), 0.02),
    }


def reference(x, c, w_ada, b_ada, norm_mix_w, w_in, w_gla_gate, b_gla_gate, gla_norm_w,
              hg_lower_bound_logits, hg_norm_w, w_out, norm_ffn_w, w_ffn_in, w_ffn_out,
              final_norm_w):
    lb_all = jnp.cumsum(jax.nn.softmax(hg_lower_bound_logits.astype(jnp.float32), axis=0), axis=0)[:DEPTH]
    c_act = jax.nn.silu(c)
    for l in range(DEPTH):
        mod = c_act @ w_ada[l] + b_ada[l]
        shift_m, scale_m, gate_m, shift_f, scale_f, gate_f = jnp.split(mod[:, None, :], 6, axis=-1)
        h = rms_norm(x, norm_mix_w[l]) * (1.0 + scale_m) + shift_m
        y = hybrid_mixer(h, w_in[l], w_gla_gate[l], b_gla_gate[l], gla_norm_w[l], lb_all[l],
                         hg_norm_w[l], w_out[l])
        x = x + gate_m * y
        h = rms_norm(x, norm_ffn_w[l]) * (1.0 + scale_f) + shift_f
        x = x + gate_f * swiglu(h, w_ffn_in[l], w_ffn_out[l])
    return rms_norm(x, final_norm_w)
```

```python
import numpy as np
from contextlib import ExitStack
import concourse.bass as bass
import concourse.mybir as mybir
from concourse.bass_utils import run_bass_kernel_spmd

F32 = mybir.dt.float32
BF16 = mybir.dt.bfloat16
AF = mybir.ActivationFunctionType
ALU = mybir.AluOpType

NCORES = 8
T = 1024
D = 2048
KC = 16
NT = 8
NCH = 16
FH = 5632
HKC = 44
EPS = 1e-6
ARENA = 144 * 1024


class Sched:
    ENG = ("pe", "act", "dve", "pool", "sp")

    def __init__(self, nc, es):
        self.nc = nc
        self.es = es
        self.sem = {k: es.enter_context(nc.semaphore("s_" + k)) for k in self.ENG}
        self.cnt = {k: 0 for k in self.ENG}
        self.seen = {k: {} for k in self.ENG}
        self.prog = {k: [] for k in self.ENG}
        self.lastw = {}
        self.readers = {}
        self.dsem = {}
        self.dcnt = {}

    def semobj(self, k):
        return self.dsem[k] if k in self.dsem else self.sem[k]

    def _waits(self, e, reads, writes):
        need = {}

        def add(k, v):
            if k == e and e == "pe":
                return
            if need.get(k, 0) < v:
                need[k] = v

        for r in reads:
            t = self.lastw.get(r)
            if t:
                add(*t)
        for w in writes:
            t = self.lastw.get(w)
            if t:
                add(*t)
            for k, v in self.readers.get(w, {}).items():
                add(k, v)
        out = []
        for k, v in need.items():
            if self.seen[e].get(k, 0) < v:
                self.seen[e][k] = v
                out.append((k, v))
        return out

    def _record(self, tok, reads, writes):
        k, v = tok
        for r in reads:
            d = self.readers.setdefault(r, {})
            if d.get(k, 0) < v:
                d[k] = v
        for w in writes:
            self.lastw[w] = tok
            self.readers[w] = {}

    def op(self, e, fn, reads=(), writes=(), signal=True):
        waits = self._waits(e, reads, writes)
        if signal:
            self.cnt[e] += 1
            tok = (e, self.cnt[e])
            inc = (e, 1)
        else:
            tok = (e, self.cnt[e] + 1)
            inc = None
        self.prog[e].append((waits, fn, inc))
        self._record(tok, reads, writes)

    def dma(self, q, dkey, fn, reads=(), writes=(), amount=16):
        waits = self._waits(q, reads, writes)
        if dkey not in self.dsem:
            self.dsem[dkey] = self.es.enter_context(self.nc.semaphore("d_" + dkey))
            self.dcnt[dkey] = 0
        self.dcnt[dkey] += (16 if amount == "cc" else amount)
        if amount == "cc":
            self.dcnt[dkey] -= 15
        tok = (dkey, self.dcnt[dkey])
        self.prog[q].append((waits, fn, (dkey, amount)))
        self._record(tok, reads, writes)
        return tok

    def settle(self, dkey, resources):
        tok = (dkey, self.dcnt[dkey])
        for r in resources:
            self.lastw[r] = tok

    def barrier(self):
        targets = [(k, self.cnt[k]) for k in self.ENG if self.cnt[k] > 0]
        targets += [(k, v) for k, v in self.dcnt.items() if v > 0]
        for e in self.ENG:
            waits = []
            for k, v in targets:
                if k == e and e in ("pe",):
                    continue
                if self.seen[e].get(k, 0) < v:
                    self.seen[e][k] = v
                    waits.append((k, v))
            if waits:
                self.prog[e].append((waits, None, None))

    def replay(self, e, eng):
        for waits, fn, inc in self.prog[e]:
            for k, v in waits:
                eng.wait_ge(self.semobj(k), v)
            if fn is None:
                continue
            ins = fn(eng)
            if inc:
                k, a = inc
                if a == "cc":
                    ins.then_inc(self.semobj(k))
                else:
                    ins.then_inc(self.semobj(k), a)


class _Stop(Exception):
    pass


NSLOT = 4


def build(stop_after=None):
    nc = bass.Bass("TRN2", target_bir_lowering=False)

    def din(name, shape):
        return nc.dram_tensor(name, list(shape), F32, kind="ExternalInput").ap()

    x_d = din("xs", [NSLOT, T, D])
    cT_d = din("cT", [128, KC])
    wada_d = din("w_ada_b", [24, 128, KC * 512])
    bada_d = din("b_adaT", [128, 96])
    nw_d = din("nwT", [128, 48])
    wglaa_d = din("w_gla_a", [4, 128, KC * 512])
    wglav_d = din("w_gla_v", [4, 128, KC * 256])
    whgp_d = din("w_hg_p", [8, 128, KC * 256])
    wglap_d = din("w_gla_p", [4, 128, KC * 384])
    whg_d = din("w_hg_u", [8, 128, KC * 512])
    wlr_d = din("w_lr", [128, KC * 16])
    wg_d = din("wg_aug", [17, 512])
    glanw_d = din("gla_nwT", [128, 2])
    hgnw_d = din("hg_nwT", [128, 1])
    lbl_d = din("lbl", [8, 128, 256])
    wout_d = din("w_out_b", [8, 128, KC * 256])
    wf1_d = din("w_f1", [22, 128, KC * 512])
    wf2_d = din("w_f2", [16, 128, HKC * 128])
    ident_d = din("ident", [128, 128])
    mask_d = din("maskT", [128, 128])
    keep_d = din("keep", [128, NSLOT])
    out_d = nc.dram_tensor("out", [T, D], F32, kind="ExternalOutput").ap()
    xscr = nc.dram_tensor("xscr", [128, KC * T], F32)

    with ExitStack() as es:
        def sb(name, shape, dt):
            return es.enter_context(nc.sbuf_tensor(name, list(shape), dt))

        ident = sb("ident_s", [128, 128], F32)
        maskf = sb("maskf", [128, 128], F32)
        identb = sb("identb", [128, 128], BF16)
        maskb = sb("maskb", [128, 128], BF16)
        ghl = sb("ghl", [128, 2 * NT * 128], BF16)
        onesb = sb("onesb", [128, 128], BF16)
        modT = sb("modT", [128, 96], F32)
        badaT = sb("badaT", [128, 96], F32)
        nwT = sb("nwT_s", [128, 48], F32)
        AB = sb("AB", [128, 64], F32)
        cT = sb("cT_s", [128, KC], F32)
        cact = sb("cact", [128, KC], BF16)
        keep = sb("keep_s", [128, NSLOT], F32)
        glanw = sb("glanw", [128, 2], F32)
        hgnw = sb("hgnw", [128, 1], F32)
        epst = sb("epst", [128, 1], F32)
        onet = sb("onet", [128, 1], F32)
        wg = sb("wg_s", [17, 512], F32)
        lrT = sb("lrT", [17, T], F32)
        wlr = sb("wlr_s", [128, KC * 16], BF16)
        ring = [sb(f"ring{i}", [128, 8192], BF16) for i in range(3)]
        arena = sb("arena", [128, ARENA // 2], BF16)
        ps = [es.enter_context(nc.psum_tensor(f"ps{i}", [128, 512], F32)) for i in range(7)]
        pst = es.enter_context(nc.psum_tensor("pst", [128, 1024], BF16))

        S = Sched(nc, es)

        def stage(name, aps):
            if stop_after != name:
                return
            S.barrier()
            for dname, ap in aps:
                d = nc.dram_tensor("dbg_" + dname, list(ap.shape), ap.dtype, kind="ExternalOutput").ap()
                S.dma("sp", "dbg", lambda e, d=d, ap=ap: e.dma_start(out=d, in_=ap), writes=("dbgo_" + dname,))
            raise _Stop()

        def av(off, nelem, dt):
            if dt is F32:
                return arena[:, off // 2: off // 2 + nelem * 2].bitcast(F32)
            return arena[:, off // 2: off // 2 + nelem]

        K = 1024
        hT = av(0, KC * T, BF16).rearrange("p (k t) -> p k t", t=T)
        xsq = [av(32 * K + i * 2 * K, T, BF16) for i in range(2)]
        tmpf = [av(36 * K + i * 4 * K, T, F32) for i in range(2)]
        rstd = av(44 * K, T, F32)
        xst = [av(48 * K + i * 8 * K, D, F32) for i in range(2)]
        xT = av(64 * K, KC * T, F32).rearrange("p (k t) -> p k t", t=T)
        xTf = av(64 * K, KC * T, F32)
        oT = av(32 * K, KC * T, BF16).rearrange("p (k t) -> p k t", t=T)
        TB0 = 64 * K
        o_ = TB0
        graw = av(o_, NT * 128, F32).rearrange("p (t d) -> p t d", d=128); o_ += 4 * K
        eb = av(o_, T, F32); o_ += 4 * K
        enb = av(o_, T, F32); o_ += 4 * K
        khf = av(o_, T, F32); o_ += 4 * K
        qf = av(o_, T, F32); o_ += 4 * K
        qt = av(o_, T, BF16); o_ += 2 * K
        kh = av(o_, T, BF16); o_ += 2 * K
        kt = av(o_, T, BF16); o_ += 2 * K
        ktok = av(o_, NT * 128, BF16).rearrange("p (t d) -> p t d", d=128); o_ += 2 * K
        vtok = av(o_, NT * 256, BF16); o_ += 4 * K
        sT = av(o_, NT * 128, BF16).rearrange("p (t i) -> p t i", i=128); o_ += 2 * K
        Sbf = av(o_, 16 * 256, BF16); o_ += 8 * K
        Sf = [av(o_ + i * K, 256, F32) for i in range(2)]; o_ += 2 * K
        small = av(o_, 128, F32); o_ += K // 2
        lbt = av(o_, 5 * 128, F32); o_ += 5 * K // 2
        osq = av(o_, T, BF16); o_ += 2 * K
        crstd = av(o_, T, F32); o_ += 4 * K
        ctmp = av(o_, T, F32); o_ += 4 * K
        assert o_ == 121 * K, o_
        oloc = av(o_, 2 * T, F32).rearrange("p (s t) -> p s t", t=T); o_ += 8 * K
        gate = av(o_, 2 * T, BF16).rearrange("p (s t) -> p s t", t=T); o_ += 4 * K
        assert o_ <= 135 * K, o_
        Sstate = av(135 * K, 2048, F32)
        Ecol = small[:, 0:16]
        bl = [small[:, 16:32], small[:, 32:48]]

        ring_state = {"n": 0}

        def wload(src_ap, ncols):
            i = ring_state["n"] % 3
            ring_state["n"] += 1
            S.dma("pool", f"ring{i}",
                  lambda e, i=i, src_ap=src_ap, ncols=ncols: e.dma_start(
                      out=(ring[i][:, 0:ncols].rearrange("p (k n) -> p k n", n=src_ap.shape[2])
                           if len(src_ap.shape) == 3 else ring[i][:, 0:ncols]), in_=src_ap),
                  reads=(), writes=(f"ring{i}",))
            return i

        abank = {"n": 0}

        def nextbank():
            b = abank["n"] % 4
            abank["n"] += 1
            return b

        def rstd_from(ps_ap, out_ap, dim, rkeys, wkey):
            S.op("act", lambda e: e.activation(out=out_ap, in_=ps_ap, func=AF.Ln, bias=epst[:], scale=1.0 / dim),
                 reads=tuple(rkeys) + ("epst",), writes=(wkey,))
            S.op("act", lambda e: e.activation(out=out_ap, in_=out_ap, func=AF.Exp, scale=-0.5),
                 reads=(wkey,), writes=(wkey,))

        mod_state = {"blk": 0}

        def mod_block():
            blk = mod_state["blk"]
            if blk >= 24:
                return
            mod_state["blk"] += 1
            slot = wload(wada_d[blk], KC * 512)
            wv = ring[slot][:, 0:KC * 512].rearrange("p (k n) -> p k n", n=512)
            for cc in range(4):
                for kc in range(KC):
                    S.op("pe", lambda e, wv=wv, kc=kc, cc=cc: e.matmul(
                        ps[6][:, cc:cc + 1], wv[:, kc, cc * 128:(cc + 1) * 128], cact[:, kc:kc + 1],
                        start=(kc == 0), stop=(kc == KC - 1)),
                        reads=(f"ring{slot}", "cact"), writes=("ps6",), signal=(kc == KC - 1))
            S.op("dve", lambda e, blk=blk: e.tensor_tensor(out=modT[:, blk * 4:(blk + 1) * 4], in0=ps[6][:, 0:4],
                                                          in1=badaT[:, blk * 4:(blk + 1) * 4], op=ALU.add),
                 reads=("ps6", "badaT"), writes=("modT",))

        def emit():
            consts = [(ident, ident_d, "ident"), (maskf, mask_d, "maskf"), (badaT, bada_d, "badaT"),
                      (nwT, nw_d, "nwT"), (cT, cT_d, "cT"), (keep, keep_d, "keep"),
                      (glanw, glanw_d, "glanw"), (hgnw, hgnw_d, "hgnw"), (wg, wg_d, "wg")]
            for t_, d_, key in consts:
                S.dma("sp", "c0", lambda e, t_=t_, d_=d_: e.dma_start(out=t_[:], in_=d_), writes=(key,))
            S.settle("c0", [c[2] for c in consts])
            S.dma("pool", "c1", lambda e: e.dma_start(out=wlr[:], in_=wlr_d), writes=("wlr",))
            S.op("dve", lambda e: e.memset(onesb[:], 1.0), writes=("onesb",))
            S.op("dve", lambda e: e.memset(epst[:], EPS), writes=("epst",))
            S.op("dve", lambda e: e.memset(onet[:], 1.0), writes=("onet",))
            S.op("dve", lambda e: e.tensor_copy(out=identb[:], in_=ident[:]), reads=("ident",), writes=("identb",))
            S.op("dve", lambda e: e.tensor_copy(out=maskb[:], in_=maskf[:]), reads=("maskf",), writes=("maskb",))
            S.op("dve", lambda e: e.memset(lrT[:], 1.0), writes=("lrT",))
            S.op("dve", lambda e: e.memset(Sstate, 0.0), writes=("Sstate",))
            S.op("act", lambda e: e.activation(out=cact[:], in_=cT[:], func=AF.Silu), reads=("cT",), writes=("cact",))
            for _ in range(8):
                mod_block()
            S.op("dve", lambda e: e.scalar_tensor_tensor(out=AB[:, 0:16], in0=modT[:, 16:32], scalar=1.0, in1=nwT[:, 0:16],
                                                         op0=ALU.add, op1=ALU.mult),
                 reads=("modT", "nwT"), writes=("AB",))
            S.op("dve", lambda e: e.tensor_copy(out=AB[:, 16:32], in_=modT[:, 0:16]), reads=("modT", "AB"), writes=("AB",))
            stage("p0", [("modT", modT[:]), ("AB", AB[:])])

            for slot in range(NSLOT):
                main = (slot == NSLOT - 1)
                phase1(slot, main)
                if slot == 0:
                    stage("p1", [("hT", av(0, KC * T, BF16)), ("rstd", rstd), ("xT", xTf)])
                S.barrier()
                lr_proj()
                if slot == 0:
                    stage("lr", [("lrT", lrT[:])])
                for g in range(4):
                    unit(slot, main, True, g, 2)
                    if main:
                        finish(2, 2 * g, glanw)
                    for j in range(2):
                        unit(slot, main, False, 2 * g + j, 1)
                        if main:
                            finish(1, 8 + 2 * g + j, hgnw)
                        if not main:
                            mod_block()
                    if not main:
                        mod_block()
                S.barrier()
                if slot == 0:
                    stage("s0", [("Sstate", Sstate)])
            while mod_state["blk"] < 24:
                mod_block()
            stage("p2", [("oT", av(32 * K, KC * T, BF16)), ("Sstate", Sstate), ("modT", modT[:])])
            S.op("dve", lambda e: e.scalar_tensor_tensor(out=AB[:, 32:48], in0=modT[:, 64:80], scalar=1.0, in1=nwT[:, 16:32],
                                                         op0=ALU.add, op1=ALU.mult),
                 reads=("modT", "nwT", "AB"), writes=("AB",))
            S.op("dve", lambda e: e.tensor_copy(out=AB[:, 48:64], in_=modT[:, 48:64]), reads=("modT", "AB"), writes=("AB",))
            S.barrier()
            phase34()

        def phase1(slot, main):
            for t in range(NT):
                sl = t % 2
                S.dma("sp", f"xs{sl}", lambda e, t=t, sl=sl: e.dma_start(out=xst[sl], in_=x_d[slot, t * 128:(t + 1) * 128, :]),
                      writes=(f"xst{sl}",))
                for g4 in range(4):
                    bank = 4 + (g4 % 2)
                    for j in range(4):
                        kc = g4 * 4 + j
                        S.op("pe", lambda e, sl=sl, kc=kc, j=j, bank=bank: e.transpose(
                            ps[bank][:, j * 128:(j + 1) * 128], xst[sl][:, kc * 128:(kc + 1) * 128], ident[:]),
                            reads=(f"xst{sl}", "ident"), writes=(f"ps{bank}",), signal=(j == 3))
                    dst = xT[:, g4 * 4:(g4 + 1) * 4, t * 128:(t + 1) * 128]
                    src = ps[bank][:].rearrange("p (a b) -> p a b", b=128)
                    if g4 % 2 == 0:
                        S.op("act", lambda e, dst=dst, src=src: e.activation(out=dst, in_=src, func=AF.Copy),
                             reads=(f"ps{bank}",), writes=(f"xT{t}a",))
                    else:
                        S.op("dve", lambda e, dst=dst, src=src: e.tensor_copy(out=dst, in_=src),
                             reads=(f"ps{bank}",), writes=(f"xT{t}d",))
            xT_all = tuple(f"xT{t}{c}" for t in range(NT) for c in "ad")
            if main:
                for q in range(4):
                    S.dma("sp", "xscrw", lambda e, q=q: e.dma_start(out=xscr.ap()[:, q * 4096:(q + 1) * 4096],
                                                                   in_=xTf[:, q * 4096:(q + 1) * 4096]),
                          reads=xT_all, writes=("xscr",))
                S.settle("xscrw", ["xscr"])
            for kc in range(KC):
                sl = kc % 2
                S.op("act", lambda e, kc=kc, sl=sl: e.activation(out=xsq[sl], in_=xT[:, kc, :], func=AF.Square),
                     reads=xT_all, writes=(f"xsq{sl}",))
                for tb in range(2):
                    S.op("pe", lambda e, kc=kc, sl=sl, tb=tb: e.matmul(
                        ps[5 + tb][:, :], onesb[:], xsq[sl][:, tb * 512:(tb + 1) * 512],
                        start=(kc == 0), stop=(kc == KC - 1)),
                        reads=(f"xsq{sl}", "onesb"), writes=(f"ps{5 + tb}",), signal=True)
            for tb in range(2):
                rstd_from(ps[5 + tb][:, :], rstd[:, tb * 512:(tb + 1) * 512], float(D), (f"ps{5 + tb}",), f"rstd{tb}")
            for kc in range(KC):
                sl = kc % 2
                S.op("dve", lambda e, kc=kc, sl=sl: e.scalar_tensor_tensor(
                    out=tmpf[sl], in0=xT[:, kc, :], scalar=AB[:, kc:kc + 1], in1=rstd, op0=ALU.mult, op1=ALU.mult),
                    reads=xT_all + ("AB", "rstd0", "rstd1"), writes=(f"tmpf{sl}",))
                S.op("act", lambda e, kc=kc, sl=sl: e.activation(
                    out=hT[:, kc, :], in_=tmpf[sl], func=AF.Identity, bias=AB[:, 16 + kc:17 + kc], scale=1.0),
                    reads=(f"tmpf{sl}", "AB"), writes=("hT",))

        def lr_proj():
            wlr3 = wlr[:].rearrange("p (k n) -> p k n", n=16)
            for tb in range(2):
                for kc in range(KC):
                    S.op("pe", lambda e, kc=kc, tb=tb: e.matmul(
                        ps[6][0:16, :], wlr3[:, kc, :], hT[:, kc, tb * 512:(tb + 1) * 512],
                        start=(kc == 0), stop=(kc == KC - 1)),
                        reads=("wlr", "hT"), writes=("ps6",), signal=(kc == KC - 1))
                S.op("act", lambda e, tb=tb: e.activation(out=lrT[0:16, tb * 512:(tb + 1) * 512], in_=ps[6][0:16, :],
                                                          func=AF.Copy), reads=("ps6",), writes=("lrT",))

        def proj_fm(slot, ncols_blk, c0, evac):
            wv = ring[slot][:, 0:KC * ncols_blk].rearrange("p (k n) -> p k n", n=ncols_blk)
            for tb in range(2):
                b = nextbank()
                for kc in range(KC):
                    S.op("pe", lambda e, wv=wv, kc=kc, tb=tb, b=b: e.matmul(
                        ps[b][:, :], wv[:, kc, c0:c0 + 128], hT[:, kc, tb * 512:(tb + 1) * 512],
                        start=(kc == 0), stop=(kc == KC - 1)),
                        reads=(f"ring{slot}", "hT"), writes=(f"ps{b}",), signal=(kc == KC - 1))
                evac(tb, b)

        def proj_tm(slot, ncols_blk, c0, n, evac):
            wv = ring[slot][:, 0:KC * ncols_blk].rearrange("p (k n) -> p k n", n=ncols_blk)
            for t in range(NT):
                b = nextbank()
                for kc in range(KC):
                    S.op("pe", lambda e, wv=wv, kc=kc, t=t, b=b: e.matmul(
                        ps[b][:, 0:n], hT[:, kc, t * 128:(t + 1) * 128], wv[:, kc, c0:c0 + n],
                        start=(kc == 0), stop=(kc == KC - 1)),
                        reads=(f"ring{slot}", "hT"), writes=(f"ps{b}",), signal=(kc == KC - 1))
                evac(t, b)

        def unit(slot_i, main, is_gla, idx, nv):
            def ust(nm, aps):
                if slot_i == 0 and is_gla and idx == 0:
                    stage(nm, aps)
                if slot_i == 0 and (not is_gla) and idx == 0:
                    stage("h" + nm[1:], aps)
            Dv = nv * 128
            scol = idx * 256 if is_gla else 1024 + idx * 128
            bs = (-1.0 / 16.0) if is_gla else 1.0
            vt3 = vtok[:, 0:NT * Dv].rearrange("p (t e) -> p t e", e=Dv)
            Sb3 = Sbf[:, 0:16 * Dv].rearrange("p (c e) -> p c e", e=Dv)
            if is_gla:
                if main:
                    slotA = wload(wglaa_d[idx], KC * 512)
                    slotB = wload(wglav_d[idx], KC * 256)
                    nA, kc0, nB, vc0 = 512, 128, 256, 0
                else:
                    slotA = wload(wglap_d[idx], KC * 384)
                    slotB = slotA
                    nA, kc0, nB, vc0 = 384, 0, 384, 128
            else:
                if main:
                    slotA = wload(whg_d[idx], KC * 512)
                    nA, fc0 = 512, 256
                else:
                    slotA = wload(whgp_d[idx], KC * 256)
                    nA, fc0 = 256, 0
            if is_gla:
                for half in range(2):
                    b = nextbank()
                    for j in range(4):
                        t = half * 4 + j
                        S.op("pe", lambda e, t=t, j=j, b=b: e.matmul(
                            ps[b][:, j * 128:(j + 1) * 128], lrT[:, t * 128:(t + 1) * 128],
                            wg[:, idx * 128:(idx + 1) * 128], start=True, stop=True),
                            reads=("lrT", "wg"), writes=(f"ps{b}",), signal=(j == 3))
                    dst = graw[:, half * 4:(half + 1) * 4, :]
                    src = ps[b][:].rearrange("p (a b) -> p a b", b=128)
                    S.op("act", lambda e, dst=dst, src=src: e.activation(out=dst, in_=src, func=AF.Exp, scale=-1.0),
                         reads=(f"ps{b}",), writes=("graw",))
                S.op("act", lambda e: e.activation(out=graw, in_=graw, func=AF.Ln, bias=onet[:], scale=1.0),
                     reads=("graw", "onet"), writes=("graw",))
            else:
                S.dma("sp", "lbl", lambda e: e.dma_start(out=lbt[:, 0:256], in_=lbl_d[idx]), writes=("lbt",))
                S.op("dve", lambda e: e.tensor_tensor(out=lbt[:, 512:640], in0=lbt[:, 128:256], in1=lbt[:, 0:128],
                                                      op=ALU.subtract), reads=("lbt",), writes=("lbt2",))
                S.op("act", lambda e: e.activation(out=lbt[:, 512:640], in_=lbt[:, 512:640], func=AF.Exp),
                     reads=("lbt2",), writes=("lbt2",))
                S.op("dve", lambda e: e.tensor_scalar(out=lbt[:, 256:384], in0=lbt[:, 512:640], scalar1=1.0, scalar2=None,
                                                      op0=ALU.add), reads=("lbt2",), writes=("lb",))
                S.op("dve", lambda e: e.reciprocal(out=lbt[:, 256:384], in_=lbt[:, 256:384]), reads=("lb",), writes=("lb",))
                S.op("dve", lambda e: e.tensor_tensor(out=lbt[:, 384:512], in0=lbt[:, 512:640], in1=lbt[:, 256:384],
                                                      op=ALU.mult), reads=("lb", "lbt2"), writes=("oml",))

                def ev_fi(t, b):
                    S.op("act", lambda e, t=t, b=b: e.activation(out=graw[:, t, :], in_=ps[b][:, 0:128], func=AF.Exp,
                                                                 scale=-1.0), reads=(f"ps{b}",), writes=("graw",))
                    S.op("act", lambda e, t=t, b=b: e.activation(out=vt3[:, t, :], in_=ps[b][:, 128:256], func=AF.Copy),
                         reads=(f"ps{b}",), writes=("vtok",))
                proj_tm(slotA, nA, fc0, 256, ev_fi)
                S.op("dve", lambda e: e.tensor_scalar(out=graw, in0=graw, scalar1=1.0, scalar2=None, op0=ALU.add),
                     reads=("graw",), writes=("graw",))
                S.op("dve", lambda e: e.reciprocal(out=graw, in_=graw), reads=("graw",), writes=("graw",))
                S.op("dve", lambda e: e.tensor_tensor(out=graw, in0=graw,
                                                      in1=lbt[:, 384:512].unsqueeze(1).to_broadcast([128, NT, 128]),
                                                      op=ALU.mult), reads=("graw", "oml"), writes=("graw",))
                S.op("dve", lambda e: e.tensor_tensor(out=graw, in0=graw,
                                                      in1=lbt[:, 256:384].unsqueeze(1).to_broadcast([128, NT, 128]),
                                                      op=ALU.add), reads=("graw", "lb"), writes=("graw",))
                S.op("act", lambda e: e.activation(out=graw, in_=graw, func=AF.Ln), reads=("graw",), writes=("graw",))
            ust("ua", [("graw", av(TB0, NT * 128, F32))])
            ghi = ghl[:, 0:NT * 128].rearrange("p (t d) -> p t d", d=128)
            glo = ghl[:, NT * 128:2 * NT * 128].rearrange("p (t d) -> p t d", d=128)
            S.op("dve", lambda e: e.tensor_copy(out=ghi, in_=graw), reads=("graw",), writes=("ghi",))
            S.op("dve", lambda e: e.tensor_tensor(out=glo, in0=graw, in1=ghi, op=ALU.subtract), reads=("graw", "ghi"), writes=("glo",))
            for half in range(2):
                for j in range(4):
                    t = half * 4 + j
                    S.op("pe", lambda e, t=t, j=j, half=half: e.matmul(
                        ps[4 + half][:, j * 128:(j + 1) * 128], ghi[:, t, :], maskb[:], start=True, stop=False),
                        reads=("ghi", "maskb"), writes=(f"ps{4 + half}",), signal=False)
                    S.op("pe", lambda e, t=t, j=j, half=half: e.matmul(
                        ps[4 + half][:, j * 128:(j + 1) * 128], glo[:, t, :], maskb[:], start=False, stop=True),
                        reads=("glo", "maskb"), writes=(f"ps{4 + half}",), signal=(j == 3))
                sl_ = slice(half * 512, (half + 1) * 512)
                if main:
                    S.op("act", lambda e, half=half, sl_=sl_: e.activation(out=eb[:, sl_], in_=ps[4 + half][:, :], func=AF.Exp,
                                                                           scale=bs), reads=(f"ps{4 + half}",), writes=("eb",))
                S.op("act", lambda e, half=half, sl_=sl_: e.activation(out=enb[:, sl_], in_=ps[4 + half][:, :], func=AF.Exp,
                                                                       scale=-bs), reads=(f"ps{4 + half}",), writes=("enb",))
            ust("ub", [("enb", enb), ("small", small)])
            S.op("dve", lambda e: e.reciprocal(out=Ecol, in_=enb.rearrange("p (c j) -> p c j", j=64)[:, :, 63]),
                 reads=("enb",), writes=("Ecol",))
            if not is_gla:
                for half in range(2):
                    b = nextbank()
                    for j in range(4):
                        t = half * 4 + j
                        S.op("pe", lambda e, t=t, j=j, b=b: e.matmul(
                            ps[b][:, j * 128:(j + 1) * 128], ghi[:, t, :], identb[:], start=True, stop=False),
                            reads=("ghi", "identb"), writes=(f"ps{b}",), signal=False)
                        S.op("pe", lambda e, t=t, j=j, b=b: e.matmul(
                            ps[b][:, j * 128:(j + 1) * 128], glo[:, t, :], identb[:], start=False, stop=True),
                            reads=("glo", "identb"), writes=(f"ps{b}",), signal=(j == 3))
                    sl_ = slice(half * 512, (half + 1) * 512)
                    S.op("act", lambda e, b=b, sl_=sl_: e.activation(out=khf[:, sl_], in_=ps[b][:, :], func=AF.Exp),
                         reads=(f"ps{b}",), writes=("khf",))
                S.op("dve", lambda e: e.tensor_scalar(out=khf, in0=khf, scalar1=-1.0, scalar2=1.0, op0=ALU.mult, op1=ALU.add),
                     reads=("khf",), writes=("khf",))
                S.op("dve", lambda e: e.tensor_tensor(out=khf, in0=khf, in1=enb, op=ALU.mult),
                     reads=("khf", "enb"), writes=("khf",))
            qscale = 128.0 ** -0.5 if is_gla else 1.0
            if is_gla:
                if main:
                    def ev_q(tb, b):
                        sl_ = slice(tb * 512, (tb + 1) * 512)
                        S.op("dve", lambda e, b=b, sl_=sl_: e.scalar_tensor_tensor(
                            out=qt[:, sl_], in0=ps[b][:, :], scalar=qscale, in1=eb[:, sl_], op0=ALU.mult, op1=ALU.mult),
                            reads=(f"ps{b}", "eb"), writes=("qt",))
                    proj_fm(slotA, nA, 0, ev_q)

                def ev_k(tb, b):
                    sl_ = slice(tb * 512, (tb + 1) * 512)
                    S.op("dve", lambda e, b=b, sl_=sl_: e.tensor_tensor(out=khf[:, sl_], in0=ps[b][:, :], in1=enb[:, sl_],
                                                                        op=ALU.mult), reads=(f"ps{b}", "enb"), writes=("khf",))
                proj_fm(slotA, nA, kc0, ev_k)
            elif main:
                def ev_q(tb, b):
                    sl_ = slice(tb * 512, (tb + 1) * 512)
                    S.op("act", lambda e, b=b, sl_=sl_: e.activation(out=qf[:, sl_], in_=ps[b][:, :], func=AF.Silu),
                         reads=(f"ps{b}",), writes=("qf",))
                    S.op("dve", lambda e, sl_=sl_: e.tensor_tensor(out=qt[:, sl_], in0=qf[:, sl_], in1=eb[:, sl_], op=ALU.mult),
                         reads=("qf", "eb"), writes=("qt",))
                proj_fm(slotA, nA, 0, ev_q)
            ust("uc", [("khf", khf), ("small", small)])
            if main:
                for s_ in range(nv):
                    c0 = (256 + s_ * 128) if is_gla else 128

                    def ev_g(tb, b, s_=s_):
                        sl_ = slice(tb * 512, (tb + 1) * 512)
                        S.op("act", lambda e, b=b, sl_=sl_: e.activation(out=gate[:, s_, sl_], in_=ps[b][:, :], func=AF.Silu),
                             reads=(f"ps{b}",), writes=("gate",))
                    proj_fm(slotA, nA, c0, ev_g)
            if is_gla:
                def ev_v(t, b):
                    S.op("act", lambda e, t=t, b=b: e.activation(out=vt3[:, t, :], in_=ps[b][:, 0:256], func=AF.Copy),
                         reads=(f"ps{b}",), writes=("vtok",))
                proj_tm(slotB, nB, vc0, 256, ev_v)
            ust("ud", [("vtok", vtok)])
            if main:
                S.op("dve", lambda e: e.tensor_copy(out=kh, in_=khf), reads=("khf",), writes=("kh",))
            S.op("dve", lambda e: e.tensor_tensor(out=kt.rearrange("p (c j) -> p c j", j=64),
                                                  in0=khf.rearrange("p (c j) -> p c j", j=64),
                                                  in1=Ecol.unsqueeze(2).to_broadcast([128, 16, 64]), op=ALU.mult),
                 reads=("khf", "Ecol"), writes=("kt",))
            for t in range(NT):
                S.op("pe", lambda e, t=t: e.transpose(pst[:, t * 128:(t + 1) * 128], kt[:, t * 128:(t + 1) * 128], identb[:]),
                     reads=("kt", "identb"), writes=("pst",), signal=(t == NT - 1))
            S.op("act", lambda e: e.activation(out=ktok, in_=pst[:].rearrange("p (t d) -> p t d", d=128), func=AF.Copy),
                 reads=("pst",), writes=("ktok",))
            ust("ue", [("ktok", av(TB0 + 26 * K, NT * 128, BF16)), ("kt", kt)])
            if main:
                for half in range(2):
                    for j in range(4):
                        t = half * 4 + j
                        S.op("pe", lambda e, t=t, j=j, half=half: e.matmul(
                            ps[4 + half][:, j * 128:(j + 1) * 128], kh[:, t * 128:(t + 1) * 128], qt[:, t * 128:(t + 1) * 128],
                            start=True, stop=True), reads=("kh", "qt"), writes=(f"ps{4 + half}",), signal=(j == 3))
                    S.op("dve", lambda e, half=half: e.tensor_tensor(
                        out=sT[:, half * 4:(half + 1) * 4, :], in0=ps[4 + half][:].rearrange("p (a b) -> p a b", b=128),
                        in1=maskf[:].unsqueeze(1).to_broadcast([128, 4, 128]), op=ALU.mult),
                        reads=(f"ps{4 + half}", "maskf"), writes=("sT",))
            S.op("dve", lambda e: e.tensor_scalar(out=Sf[0][:, 0:Dv], in0=Sstate[:, scol:scol + Dv],
                                                  scalar1=keep[:, slot_i:slot_i + 1], scalar2=None, op0=ALU.mult),
                 reads=("Sstate", "keep"), writes=("Sf0",))
            if main:
                S.op("dve", lambda e: e.tensor_copy(out=Sb3[:, 0, :], in_=Sf[0][:, 0:Dv]), reads=("Sf0",), writes=("Sbf",))
            cpb = 512 // Dv
            for c0_ in range(0, NCH, 2 * cpb):
                bx, by = nextbank(), nextbank()
                loc = {}
                for j in range(2 * cpb):
                    c = c0_ + j
                    t, r0 = c // 2, (c % 2) * 64
                    bb = bx if c % 2 == 0 else by
                    col = (j // 2) * Dv
                    loc[c] = (bb, col)
                    S.op("pe", lambda e, t=t, r0=r0, bb=bb, col=col: e.matmul(
                        ps[bb][:, col:col + Dv], ktok[r0:r0 + 64, t, :], vt3[r0:r0 + 64, t, :], start=True, stop=True),
                        reads=("ktok", "vtok"), writes=(f"ps{bx}", f"ps{by}"), signal=(j == 2 * cpb - 1))
                for j in range(2 * cpb):
                    c = c0_ + j
                    bb, col = loc[c]
                    a_, b_ = Sf[c % 2], Sf[(c + 1) % 2]
                    if main and c < NCH - 1:
                        S.op("dve", lambda e, c=c, bb=bb, col=col, a_=a_: e.scalar_tensor_tensor(
                            out=Sb3[:, c + 1, :], in0=a_[:, 0:Dv], scalar=Ecol[:, c:c + 1], in1=ps[bb][:, col:col + Dv],
                            op0=ALU.mult, op1=ALU.add), reads=(f"Sf{c % 2}", "Ecol", f"ps{bb}"), writes=("Sbf",))
                    S.op("dve", lambda e, c=c, bb=bb, col=col, a_=a_, b_=b_: e.scalar_tensor_tensor(
                        out=b_[:, 0:Dv], in0=a_[:, 0:Dv], scalar=Ecol[:, c:c + 1], in1=ps[bb][:, col:col + Dv],
                        op0=ALU.mult, op1=ALU.add), reads=(f"Sf{c % 2}", "Ecol", f"ps{bb}"), writes=(f"Sf{(c + 1) % 2}",))
            S.op("dve", lambda e: e.tensor_copy(out=Sstate[:, scol:scol + Dv], in_=Sf[0][:, 0:Dv]),
                 reads=("Sf0", "Sstate"), writes=("Sstate",))
            ust("uf", [("Sstate", Sstate)])
            if not main:
                return
            for s_ in range(nv):
                bx, by = nextbank(), nextbank()
                for t in range(NT):
                    for cb in range(2):
                        r0 = cb * 64
                        c = 2 * t + cb
                        bb = bx if cb == 0 else by
                        o_ap = ps[bb][:, t * 64:(t + 1) * 64]
                        S.op("pe", lambda e, t=t, r0=r0, o_ap=o_ap, s_=s_: e.matmul(
                            o_ap, vt3[r0:r0 + 64, t, s_ * 128:(s_ + 1) * 128], sT[r0:r0 + 64, t, r0:r0 + 64],
                            start=True, stop=False), reads=("vtok", "sT"), writes=(f"ps{bx}", f"ps{by}"), signal=False)
                        S.op("pe", lambda e, t=t, r0=r0, o_ap=o_ap, c=c, s_=s_: e.matmul(
                            o_ap, Sb3[:, c, s_ * 128:(s_ + 1) * 128], qt[:, t * 128 + r0: t * 128 + r0 + 64],
                            start=False, stop=True), reads=("Sbf", "qt"), writes=(f"ps{bx}", f"ps{by}"),
                            signal=(t == NT - 1 and cb == 1))
                ol4 = oloc[:, s_, :].rearrange("p (t c j) -> p t c j", c=2, j=64)
                S.op("act", lambda e, bx=bx, ol4=ol4: e.activation(
                    out=ol4[:, :, 0, :], in_=ps[bx][:].rearrange("p (t j) -> p t j", j=64), func=AF.Copy),
                    reads=(f"ps{bx}",), writes=("oloc",))
                S.op("act", lambda e, by=by, ol4=ol4: e.activation(
                    out=ol4[:, :, 1, :], in_=ps[by][:].rearrange("p (t j) -> p t j", j=64), func=AF.Copy),
                    reads=(f"ps{by}", "oloc"), writes=("oloc",))
            stage("u%d" % (idx if is_gla else 4 + idx), [("oloc", av(121 * K, 2 * T, F32)), ("qt", qt), ("eb", eb),
                  ("enb", enb), ("khf", khf), ("graw", av(TB0, NT * 128, F32)), ("vtok", vtok), ("Sstate", Sstate),
                  ("small", small), ("lrT", lrT[:]), ("Sbf", Sbf)])

        def finish(nv, et0, nwt):
            for s_ in range(nv):
                S.op("act", lambda e, s_=s_: e.activation(out=osq, in_=oloc[:, s_, :], func=AF.Square),
                     reads=("oloc",), writes=("osq",))
                for tb in range(2):
                    S.op("pe", lambda e, tb=tb, s_=s_: e.matmul(
                        ps[5 + tb][:, :], onesb[:], osq[:, tb * 512:(tb + 1) * 512], start=(s_ == 0), stop=(s_ == nv - 1)),
                        reads=("osq", "onesb"), writes=(f"ps{5 + tb}",))
            for tb in range(2):
                rstd_from(ps[5 + tb][:, :], crstd[:, tb * 512:(tb + 1) * 512], float(nv * 128), (f"ps{5 + tb}",), f"crstd{tb}")
            for s_ in range(nv):
                S.op("dve", lambda e, s_=s_: e.scalar_tensor_tensor(
                    out=ctmp, in0=oloc[:, s_, :], scalar=nwt[:, s_:s_ + 1], in1=crstd, op0=ALU.mult, op1=ALU.mult),
                    reads=("oloc", "crstd0", "crstd1", "glanw", "hgnw"), writes=("ctmp",))
                S.op("dve", lambda e, s_=s_: e.tensor_tensor(out=oT[:, et0 + s_, :], in0=ctmp, in1=gate[:, s_, :],
                                                             op=ALU.mult), reads=("ctmp", "gate"), writes=("oT",))

        def phase34():
            x1T = av(64 * K, KC * T, F32).rearrange("p (k t) -> p k t", t=T)
            x1f = av(64 * K, KC * T, F32)
            sq2 = [av(128 * K + i * K, 512, BF16) for i in range(4)]
            for q in range(4):
                S.dma("sp", "xscrr", lambda e, q=q: e.dma_start(out=x1f[:, q * 4096:(q + 1) * 4096],
                                                               in_=xscr.ap()[:, q * 4096:(q + 1) * 4096]),
                      reads=("xscr",), writes=("x1T",))
            S.settle("xscrr", ["x1T"])
            sqn = {"n": 0}

            def resid_evac(b, dct, tb, gcol, ssbank, first, last):
                sl_ = slice(tb * 512, (tb + 1) * 512)
                S.op("dve", lambda e: e.scalar_tensor_tensor(
                    out=x1T[:, dct, sl_], in0=ps[b][:, :], scalar=modT[:, gcol:gcol + 1], in1=x1T[:, dct, sl_],
                    op0=ALU.mult, op1=ALU.add), reads=(f"ps{b}", "modT", "x1T"), writes=("x1T",))
                i = sqn["n"] % 4
                sqn["n"] += 1
                S.op("act", lambda e, i=i: e.activation(out=sq2[i], in_=x1T[:, dct, sl_], func=AF.Square),
                     reads=("x1T",), writes=(f"sq2{i}",))
                S.op("pe", lambda e, i=i: e.matmul(ps[ssbank][:, :], onesb[:], sq2[i], start=first, stop=last),
                     reads=(f"sq2{i}", "onesb"), writes=(f"ps{ssbank}",))

            for wb in range(8):
                slot = wload(wout_d[wb], KC * 256)
                wv = ring[slot][:, 0:KC * 256].rearrange("p (k n) -> p k n", n=256)
                for ct in range(2):
                    dct = wb * 2 + ct
                    for tb in range(2):
                        b = nextbank()
                        for kc in range(KC):
                            S.op("pe", lambda e, wv=wv, kc=kc, ct=ct, tb=tb, b=b: e.matmul(
                                ps[b][:, :], wv[:, kc, ct * 128:(ct + 1) * 128], oT[:, kc, tb * 512:(tb + 1) * 512],
                                start=(kc == 0), stop=(kc == KC - 1)),
                                reads=(f"ring{slot}", "oT"), writes=(f"ps{b}",), signal=(kc == KC - 1))
                        resid_evac(b, dct, tb, 32 + dct, 5 + tb, dct == 0, dct == KC - 1)
            stage("p3", [("x1T", x1f)])

            hid = av(0, HKC * 512, BF16).rearrange("p (k t) -> p k t", t=512)
            h2T = av(44 * K, KC * 512, BF16).rearrange("p (k t) -> p k t", t=512)
            rstd2 = av(60 * K, 512, F32)
            tmp2 = av(62 * K, 512, F32)
            sa = [av(132 * K + i * 2 * K, 512, F32) for i in range(2)]
            ost = [av(136 * K + i * 4 * K, 1024, F32) for i in range(2)]
            for tb in range(2):
                sl_ = slice(tb * 512, (tb + 1) * 512)
                rstd_from(ps[5 + tb][:, :], rstd2, float(D), (f"ps{5 + tb}",), "rstd2")
                for kc in range(KC):
                    S.op("dve", lambda e, kc=kc, sl_=sl_: e.scalar_tensor_tensor(
                        out=tmp2, in0=x1T[:, kc, sl_], scalar=AB[:, 32 + kc:33 + kc], in1=rstd2, op0=ALU.mult, op1=ALU.mult),
                        reads=("x1T", "AB", "rstd2"), writes=("tmp2",))
                    S.op("act", lambda e, kc=kc: e.activation(out=h2T[:, kc, :], in_=tmp2, func=AF.Identity,
                                                              bias=AB[:, 48 + kc:49 + kc], scale=1.0),
                         reads=("tmp2", "AB"), writes=("h2T",))
                for fb in range(22):
                    slot = wload(wf1_d[fb], KC * 512)
                    wv = ring[slot][:, 0:KC * 512].rearrange("p (k n) -> p k n", n=512)
                    for hh in range(2):
                        ht = fb * 2 + hh
                        ba, bu = nextbank(), nextbank()
                        for (bb, c0) in ((ba, hh * 128), (bu, 256 + hh * 128)):
                            for kc in range(KC):
                                S.op("pe", lambda e, wv=wv, kc=kc, bb=bb, c0=c0: e.matmul(
                                    ps[bb][:, :], wv[:, kc, c0:c0 + 128], h2T[:, kc, :], start=(kc == 0), stop=(kc == KC - 1)),
                                    reads=(f"ring{slot}", "h2T"), writes=(f"ps{bb}",), signal=(kc == KC - 1))
                        i = ht % 2
                        S.op("act", lambda e, ba=ba, i=i: e.activation(out=sa[i], in_=ps[ba][:, :], func=AF.Silu),
                             reads=(f"ps{ba}",), writes=(f"sa{i}",))
                        S.op("dve", lambda e, bu=bu, i=i, ht=ht: e.tensor_tensor(out=hid[:, ht, :], in0=ps[bu][:, :], in1=sa[i],
                                                                                op=ALU.mult), reads=(f"ps{bu}", f"sa{i}"), writes=("hid",))
                for dct in range(KC):
                    slot = wload(wf2_d[dct], HKC * 128)
                    wv = ring[slot][:, 0:HKC * 128].rearrange("p (k n) -> p k n", n=128)
                    b = nextbank()
                    for kc in range(HKC):
                        S.op("pe", lambda e, wv=wv, kc=kc, b=b: e.matmul(
                            ps[b][:, :], wv[:, kc, :], hid[:, kc, :], start=(kc == 0), stop=(kc == HKC - 1)),
                            reads=(f"ring{slot}", "hid"), writes=(f"ps{b}",), signal=(kc == HKC - 1))
                    resid_evac(b, dct, tb, 80 + dct, 4, dct == 0, dct == KC - 1)
                rstd_from(ps[4][:, :], rstd2, float(D), ("ps4",), "rstd2")
                for kc in range(KC):
                    S.op("dve", lambda e, kc=kc, sl_=sl_: e.scalar_tensor_tensor(
                        out=x1T[:, kc, sl_], in0=x1T[:, kc, sl_], scalar=nwT[:, 32 + kc:33 + kc], in1=rstd2,
                        op0=ALU.mult, op1=ALU.mult), reads=("x1T", "nwT", "rstd2"), writes=("x1T",))
                for tt in range(4):
                    t = tb * 4 + tt
                    for hf in range(2):
                        i = hf
                        for q in range(2):
                            b = nextbank()
                            for j in range(4):
                                kc = hf * 8 + q * 4 + j
                                S.op("pe", lambda e, kc=kc, j=j, b=b, t=t: e.transpose(
                                    ps[b][:, j * 128:(j + 1) * 128], x1T[:, kc, t * 128:(t + 1) * 128], ident[:]),
                                    reads=("x1T", "ident"), writes=(f"ps{b}",), signal=(j == 3))
                            if q == 0:
                                S.op("act", lambda e, b=b, i=i: e.activation(out=ost[i][:, 0:512], in_=ps[b][:, :], func=AF.Copy),
                                     reads=(f"ps{b}",), writes=(f"ost{i}",))
                            else:
                                S.op("dve", lambda e, b=b, i=i: e.tensor_copy(out=ost[i][:, 512:1024], in_=ps[b][:, :]),
                                     reads=(f"ps{b}",), writes=(f"ost{i}",))
                        S.dma("sp", f"ost{i}", lambda e, t=t, hf=hf, i=i: e.dma_start(
                            out=out_d[t * 128:(t + 1) * 128, hf * 1024:(hf + 1) * 1024], in_=ost[i]),
                            reads=(f"ost{i}",), writes=("outd",))

        try:
            emit()
        except _Stop:
            pass
        S.barrier()

        with nc.Block() as block:
            @block.tensor
            def _(eng):
                S.replay("pe", eng)

            @block.scalar
            def _(eng):
                S.replay("act", eng)

            @block.vector
            def _(eng):
                S.replay("dve", eng)

            @block.gpsimd
            def _(eng):
                S.replay("pool", eng)

            @block.sync
            def _(eng):
                S.replay("sp", eng)
    return nc


def _prep_inputs(x, c, w_ada, b_ada, norm_mix_w, w_in, w_gla_gate, b_gla_gate, gla_norm_w,
                 hg_lower_bound_logits, hg_norm_w, w_out, norm_ffn_w, w_ffn_in, w_ffn_out, final_norm_w):
    f = lambda a: np.ascontiguousarray(np.asarray(a, dtype=np.float32))
    x, c, w_ada, b_ada, w_in = f(x), f(c), f(w_ada)[0], f(b_ada)[0], f(w_in)[0]
    w_gla_gate, b_gla_gate = f(w_gla_gate)[0], f(b_gla_gate)[0]
    w_out, w_ffn_in, w_ffn_out = f(w_out)[0], f(w_ffn_in)[0], f(w_ffn_out)[0]
    lbl = f(hg_lower_bound_logits)

    def pk(v, n):
        return np.ascontiguousarray(v.reshape(n, 128).T)

    def blk(w, cols):
        kc = w.shape[0] // 128
        return np.ascontiguousarray(w[:, cols].reshape(kc, 128, len(cols)).transpose(1, 0, 2).reshape(128, -1))

    ar = np.arange
    shared = {}
    shared["b_adaT"] = pk(b_ada, 96)
    shared["nwT"] = np.ascontiguousarray(np.concatenate(
        [pk(f(norm_mix_w)[0], 16), pk(f(norm_ffn_w)[0], 16), pk(f(final_norm_w), 16)], axis=1))
    shared["w_gla_a"] = np.stack([blk(w_in, np.concatenate([ar(h * 128, (h + 1) * 128), 512 + ar(h * 128, (h + 1) * 128),
                                                            2048 + ar(h * 256, (h + 1) * 256)])) for h in range(4)])
    shared["w_gla_v"] = np.stack([blk(w_in, 1024 + ar(h * 256, (h + 1) * 256)) for h in range(4)])
    shared["w_hg_p"] = np.stack([blk(w_in, np.concatenate([4112 + ar(j * 128, (j + 1) * 128), 5136 + ar(j * 128, (j + 1) * 128)]))
                                 for j in range(8)])
    shared["w_gla_p"] = np.stack([blk(w_in, np.concatenate([512 + ar(h * 128, (h + 1) * 128), 1024 + ar(h * 256, (h + 1) * 256)]))
                                  for h in range(4)])
    shared["w_ada_b"] = np.stack([blk(w_ada, ar(q * 512, (q + 1) * 512)) for q in range(24)])
    hg = []
    for j in range(8):
        cols = np.concatenate([3088 + ar(j * 128, (j + 1) * 128), 6160 + ar(j * 128, (j + 1) * 128),
                               4112 + ar(j * 128, (j + 1) * 128), 5136 + ar(j * 128, (j + 1) * 128)])
        hg.append(blk(w_in, cols))
    shared["w_hg_u"] = np.stack(hg)
    shared["w_lr"] = blk(w_in, ar(3072, 3088))
    shared["wg_aug"] = np.ascontiguousarray(np.concatenate([w_gla_gate, b_gla_gate[None, :]], axis=0))
    shared["gla_nwT"] = pk(f(gla_norm_w)[0], 2)
    shared["hg_nwT"] = pk(f(hg_norm_w)[0], 1)
    lb = np.stack([np.concatenate([lbl[0, j * 128:(j + 1) * 128], lbl[1, j * 128:(j + 1) * 128]]) for j in range(8)])
    shared["lbl"] = np.ascontiguousarray(np.broadcast_to(lb[:, None, :], (8, 128, 256)))
    shared["w_out_b"] = np.stack([blk(w_out, ar(b * 256, (b + 1) * 256)) for b in range(8)])
    shared["w_f1"] = np.stack([blk(w_ffn_in, np.concatenate([ar(b * 256, (b + 1) * 256), FH + ar(b * 256, (b + 1) * 256)]))
                               for b in range(22)])
    shared["w_f2"] = np.stack([blk(w_ffn_out, ar(b * 128, (b + 1) * 128)) for b in range(16)])
    shared["ident"] = np.eye(128, dtype=np.float32)
    j = ar(128)[:, None]
    i = ar(128)[None, :]
    shared["maskT"] = ((j // 64 == i // 64) & (j <= i)).astype(np.float32)
    in_maps = []
    for r in range(NCORES):
        b, s = r // 4, r % 4
        m = dict(shared)
        segs = [max(j - (3 - s), 0) for j in range(3)] + [s]
        m["xs"] = np.ascontiguousarray(np.stack([x[b, q * T:(q + 1) * T, :] for q in segs]))
        m["cT"] = pk(c[b], 16)
        kp = np.ones((128, NSLOT), np.float32)
        kp[:, 3 - s] = 0.0
        m["keep"] = kp
        in_maps.append(m)
    return in_maps


_NC_CACHE = {}


def kernel(**inputs):
    in_maps = _prep_inputs(**inputs)
    if "nc" not in _NC_CACHE:
        _NC_CACHE["nc"] = build()
    res = run_bass_kernel_spmd(_NC_CACHE["nc"], in_maps, core_ids=list(range(NCORES)))
    out = np.empty((2, 4096, D), np.float32)
    for r in range(NCORES):
        b, s = r // 4, r % 4
        out[b, s * T:(s + 1) * T, :] = res.results[r]["out"]
    return out
```

```python
import numpy as np
from contextlib import ExitStack
import concourse.bass as bass
import concourse.mybir as mybir
from concourse.bass_utils import run_bass_kernel_spmd

F32 = mybir.dt.float32
BF16 = mybir.dt.bfloat16
AF = mybir.ActivationFunctionType
ALU = mybir.AluOpType

NCORES = 8
T = 1024
D = 2048
KC = 16
NT = 8
NCH = 16
FH = 5632
HKC = 44
EPS = 1e-6
ARENA = 144 * 1024


class Sched:
    ENG = ("pe", "act", "dve", "pool", "sp")

    def __init__(self, nc, es):
        self.nc = nc
        self.es = es
        self.sem = {k: es.enter_context(nc.semaphore("s_" + k)) for k in self.ENG}
        self.cnt = {k: 0 for k in self.ENG}
        self.seen = {k: {} for k in self.ENG}
        self.prog = {k: [] for k in self.ENG}
        self.lastw = {}
        self.readers = {}
        self.dsem = {}
        self.dcnt = {}

    def semobj(self, k):
        return self.dsem[k] if k in self.dsem else self.sem[k]

    def _waits(self, e, reads, writes):
        need = {}

        def add(k, v):
            if k == e and e == "pe":
                return
            if need.get(k, 0) < v:
                need[k] = v

        for r in reads:
            t = self.lastw.get(r)
            if t:
                add(*t)
        for w in writes:
            t = self.lastw.get(w)
            if t:
                add(*t)
            for k, v in self.readers.get(w, {}).items():
                add(k, v)
        out = []
        for k, v in need.items():
            if self.seen[e].get(k, 0) < v:
                self.seen[e][k] = v
                out.append((k, v))
        return out

    def _record(self, tok, reads, writes):
        k, v = tok
        for r in reads:
            d = self.readers.setdefault(r, {})
            if d.get(k, 0) < v:
                d[k] = v
        for w in writes:
            self.lastw[w] = tok
            self.readers[w] = {}

    def op(self, e, fn, reads=(), writes=(), signal=True):
        waits = self._waits(e, reads, writes)
        if signal:
            self.cnt[e] += 1
            tok = (e, self.cnt[e])
            inc = (e, 1)
        else:
            tok = (e, self.cnt[e] + 1)
            inc = None
        self.prog[e].append((waits, fn, inc))
        self._record(tok, reads, writes)

    def dma(self, q, dkey, fn, reads=(), writes=(), amount=16):
        waits = self._waits(q, reads, writes)
        if dkey not in self.dsem:
            self.dsem[dkey] = self.es.enter_context(self.nc.semaphore("d_" + dkey))
            self.dcnt[dkey] = 0
        self.dcnt[dkey] += (16 if amount == "cc" else amount)
        if amount == "cc":
            self.dcnt[dkey] -= 15
        tok = (dkey, self.dcnt[dkey])
        self.prog[q].append((waits, fn, (dkey, amount)))
        self._record(tok, reads, writes)
        return tok

    def settle(self, dkey, resources):
        tok = (dkey, self.dcnt[dkey])
        for r in resources:
            self.lastw[r] = tok

    def barrier(self):
        targets = [(k, self.cnt[k]) for k in self.ENG if self.cnt[k] > 0]
        targets += [(k, v) for k, v in self.dcnt.items() if v > 0]
        for e in self.ENG:
            waits = []
            for k, v in targets:
                if k == e and e in ("pe",):
                    continue
                if self.seen[e].get(k, 0) < v:
                    self.seen[e][k] = v
                    waits.append((k, v))
            if waits:
                self.prog[e].append((waits, None, None))

    def replay(self, e, eng):
        for waits, fn, inc in self.prog[e]:
            for k, v in waits:
                eng.wait_ge(self.semobj(k), v)
            if fn is None:
                continue
            ins = fn(eng)
            if inc:
                k, a = inc
                if a == "cc":
                    ins.then_inc(self.semobj(k))
                else:
                    ins.then_inc(self.semobj(k), a)


class _Stop(Exception):
    pass


NSLOT = 4


def build(stop_after=None):
    nc = bass.Bass("TRN2", target_bir_lowering=False)

    def din(name, shape):
        return nc.dram_tensor(name, list(shape), F32, kind="ExternalInput").ap()

    x_d = din("xs", [NSLOT, T, D])
    cT_d = din("cT", [128, KC])
    wada_d = din("w_ada_b", [24, 128, KC * 512])
    bada_d = din("b_adaT", [128, 96])
    nw_d = din("nwT", [128, 48])
    wglaa_d = din("w_gla_a", [4, 128, KC * 512])
    wglav_d = din("w_gla_v", [4, 128, KC * 256])
    whgp_d = din("w_hg_p", [8, 128, KC * 256])
    wglap_d = din("w_gla_p", [4, 128, KC * 384])
    whg_d = din("w_hg_u", [8, 128, KC * 512])
    wlr_d = din("w_lr", [128, KC * 16])
    wg_d = din("wg_aug", [17, 512])
    glanw_d = din("gla_nwT", [128, 2])
    hgnw_d = din("hg_nwT", [128, 1])
    lbl_d = din("lbl", [8, 128, 256])
    wout_d = din("w_out_b", [8, 128, KC * 256])
    wf1_d = din("w_f1", [22, 128, KC * 512])
    wf2_d = din("w_f2", [16, 128, HKC * 128])
    ident_d = din("ident", [128, 128])
    mask_d = din("maskT", [128, 128])
    keep_d = din("keep", [128, NSLOT])
    out_d = nc.dram_tensor("out", [T, D], F32, kind="ExternalOutput").ap()
    xscr = nc.dram_tensor("xscr", [128, KC * T], F32)

    with ExitStack() as es:
        def sb(name, shape, dt):
            return es.enter_context(nc.sbuf_tensor(name, list(shape), dt))

        ident = sb("ident_s", [128, 128], F32)
        maskf = sb("maskf", [128, 128], F32)
        identb = sb("identb", [128, 128], BF16)
        maskb = sb("maskb", [128, 128], BF16)
        ghl = sb("ghl", [128, 2 * NT * 128], BF16)
        onesb = sb("onesb", [128, 128], BF16)
        modT = sb("modT", [128, 96], F32)
        badaT = sb("badaT", [128, 96], F32)
        nwT = sb("nwT_s", [128, 48], F32)
        AB = sb("AB", [128, 64], F32)
        cT = sb("cT_s", [128, KC], F32)
        cact = sb("cact", [128, KC], BF16)
        keep = sb("keep_s", [128, NSLOT], F32)
        glanw = sb("glanw", [128, 2], F32)
        hgnw = sb("hgnw", [128, 1], F32)
        epst = sb("epst", [128, 1], F32)
        onet = sb("onet", [128, 1], F32)
        wg = sb("wg_s", [17, 512], F32)
        lrT = sb("lrT", [17, T], F32)
        wlr = sb("wlr_s", [128, KC * 16], BF16)
        ring = [sb(f"ring{i}", [128, 8192], BF16) for i in range(3)]
        arena = sb("arena", [128, ARENA // 2], BF16)
        ps = [es.enter_context(nc.psum_tensor(f"ps{i}", [128, 512], F32)) for i in range(7)]
        pst = es.enter_context(nc.psum_tensor("pst", [128, 1024], BF16))

        S = Sched(nc, es)

        def stage(name, aps):
            if stop_after != name:
                return
            S.barrier()
            for dname, ap in aps:
                d = nc.dram_tensor("dbg_" + dname, list(ap.shape), ap.dtype, kind="ExternalOutput").ap()
                S.dma("sp", "dbg", lambda e, d=d, ap=ap: e.dma_start(out=d, in_=ap), writes=("dbgo_" + dname,))
            raise _Stop()

        def av(off, nelem, dt):
            if dt is F32:
                return arena[:, off // 2: off // 2 + nelem * 2].bitcast(F32)
            return arena[:, off // 2: off // 2 + nelem]

        K = 1024
        hT = av(0, KC * T, BF16).rearrange("p (k t) -> p k t", t=T)
        xsq = [av(32 * K + i * 2 * K, T, BF16) for i in range(2)]
        tmpf = [av(36 * K + i * 4 * K, T, F32) for i in range(2)]
        rstd = av(44 * K, T, F32)
        xst = [av(48 * K + i * 8 * K, D, F32) for i in range(2)]
        xT = av(64 * K, KC * T, F32).rearrange("p (k t) -> p k t", t=T)
        xTf = av(64 * K, KC * T, F32)
        oT = av(32 * K, KC * T, BF16).rearrange("p (k t) -> p k t", t=T)
        TB0 = 64 * K
        o_ = TB0
        graw = av(o_, NT * 128, F32).rearrange("p (t d) -> p t d", d=128); o_ += 4 * K
        eb = av(o_, T, F32); o_ += 4 * K
        enb = av(o_, T, F32); o_ += 4 * K
        khf = av(o_, T, F32); o_ += 4 * K
        qf = av(o_, T, F32); o_ += 4 * K
        qt = av(o_, T, BF16); o_ += 2 * K
        kh = av(o_, T, BF16); o_ += 2 * K
        kt = av(o_, T, BF16); o_ += 2 * K
        ktok = av(o_, NT * 128, BF16).rearrange("p (t d) -> p t d", d=128); o_ += 2 * K
        vtok = av(o_, NT * 256, BF16); o_ += 4 * K
        sT = av(o_, NT * 128, BF16).rearrange("p (t i) -> p t i", i=128); o_ += 2 * K
        Sbf = av(o_, 16 * 256, BF16); o_ += 8 * K
        Sf = [av(o_ + i * K, 256, F32) for i in range(2)]; o_ += 2 * K
        small = av(o_, 128, F32); o_ += K // 2
        lbt = av(o_, 5 * 128, F32); o_ += 5 * K // 2
        osq = av(o_, T, BF16); o_ += 2 * K
        crstd = av(o_, T, F32); o_ += 4 * K
        ctmp = av(o_, T, F32); o_ += 4 * K
        assert o_ == 121 * K, o_
        oloc = av(o_, 2 * T, F32).rearrange("p (s t) -> p s t", t=T); o_ += 8 * K
        gate = av(o_, 2 * T, BF16).rearrange("p (s t) -> p s t", t=T); o_ += 4 * K
        assert o_ <= 135 * K, o_
        Sstate = av(135 * K, 2048, F32)
        Ecol = small[:, 0:16]
        bl = [small[:, 16:32], small[:, 32:48]]

        ring_state = {"n": 0}

        def wload(src_ap, ncols):
            i = ring_state["n"] % 3
            ring_state["n"] += 1
            S.dma("pool", f"ring{i}",
                  lambda e, i=i, src_ap=src_ap, ncols=ncols: e.dma_start(
                      out=(ring[i][:, 0:ncols].rearrange("p (k n) -> p k n", n=src_ap.shape[2])
                           if len(src_ap.shape) == 3 else ring[i][:, 0:ncols]), in_=src_ap),
                  reads=(), writes=(f"ring{i}",))
            return i

        abank = {"n": 0}

        def nextbank():
            b = abank["n"] % 4
            abank["n"] += 1
            return b

        def rstd_from(ps_ap, out_ap, dim, rkeys, wkey):
            S.op("act", lambda e: e.activation(out=out_ap, in_=ps_ap, func=AF.Ln, bias=epst[:], scale=1.0 / dim),
                 reads=tuple(rkeys) + ("epst",), writes=(wkey,))
            S.op("act", lambda e: e.activation(out=out_ap, in_=out_ap, func=AF.Exp, scale=-0.5),
                 reads=(wkey,), writes=(wkey,))

        mod_state = {"blk": 0}

        def mod_block():
            blk = mod_state["blk"]
            if blk >= 24:
                return
            mod_state["blk"] += 1
            slot = wload(wada_d[blk], KC * 512)
            wv = ring[slot][:, 0:KC * 512].rearrange("p (k n) -> p k n", n=512)
            for cc in range(4):
                for kc in range(KC):
                    S.op("pe", lambda e, wv=wv, kc=kc, cc=cc: e.matmul(
                        ps[6][:, cc:cc + 1], wv[:, kc, cc * 128:(cc + 1) * 128], cact[:, kc:kc + 1],
                        start=(kc == 0), stop=(kc == KC - 1)),
                        reads=(f"ring{slot}", "cact"), writes=("ps6",), signal=(kc == KC - 1))
            S.op("dve", lambda e, blk=blk: e.tensor_tensor(out=modT[:, blk * 4:(blk + 1) * 4], in0=ps[6][:, 0:4],
                                                          in1=badaT[:, blk * 4:(blk + 1) * 4], op=ALU.add),
                 reads=("ps6", "badaT"), writes=("modT",))

        def emit():
            consts = [(ident, ident_d, "ident"), (maskf, mask_d, "maskf"), (badaT, bada_d, "badaT"),
                      (nwT, nw_d, "nwT"), (cT, cT_d, "cT"), (keep, keep_d, "keep"),
                      (glanw, glanw_d, "glanw"), (hgnw, hgnw_d, "hgnw"), (wg, wg_d, "wg")]
            for t_, d_, key in consts:
                S.dma("sp", "c0", lambda e, t_=t_, d_=d_: e.dma_start(out=t_[:], in_=d_), writes=(key,))
            S.settle("c0", [c[2] for c in consts])
            S.dma("pool", "c1", lambda e: e.dma_start(out=wlr[:], in_=wlr_d), writes=("wlr",))
            S.op("dve", lambda e: e.memset(onesb[:], 1.0), writes=("onesb",))
            S.op("dve", lambda e: e.memset(epst[:], EPS), writes=("epst",))
            S.op("dve", lambda e: e.memset(onet[:], 1.0), writes=("onet",))
            S.op("dve", lambda e: e.tensor_copy(out=identb[:], in_=ident[:]), reads=("ident",), writes=("identb",))
            S.op("dve", lambda e: e.tensor_copy(out=maskb[:], in_=maskf[:]), reads=("maskf",), writes=("maskb",))
            S.op("dve", lambda e: e.memset(lrT[:], 1.0), writes=("lrT",))
            S.op("dve", lambda e: e.memset(Sstate, 0.0), writes=("Sstate",))
            S.op("act", lambda e: e.activation(out=cact[:], in_=cT[:], func=AF.Silu), reads=("cT",), writes=("cact",))
            for _ in range(8):
                mod_block()
            S.op("dve", lambda e: e.scalar_tensor_tensor(out=AB[:, 0:16], in0=modT[:, 16:32], scalar=1.0, in1=nwT[:, 0:16],
                                                         op0=ALU.add, op1=ALU.mult),
                 reads=("modT", "nwT"), writes=("AB",))
            S.op("dve", lambda e: e.tensor_copy(out=AB[:, 16:32], in_=modT[:, 0:16]), reads=("modT", "AB"), writes=("AB",))
            stage("p0", [("modT", modT[:]), ("AB", AB[:])])

            for slot in range(NSLOT):
                main = (slot == NSLOT - 1)
                phase1(slot, main)
                if slot == 0:
                    stage("p1", [("hT", av(0, KC * T, BF16)), ("rstd", rstd), ("xT", xTf)])
                S.barrier()
                lr_proj()
                if slot == 0:
                    stage("lr", [("lrT", lrT[:])])
                for g in range(4):
                    unit(slot, main, True, g, 2)
                    if main:
                        finish(2, 2 * g, glanw)
                    for j in range(2):
                        unit(slot, main, False, 2 * g + j, 1)
                        if main:
                            finish(1, 8 + 2 * g + j, hgnw)
                        if not main:
                            mod_block()
                    if not main:
                        mod_block()
                S.barrier()
                if slot == 0:
                    stage("s0", [("Sstate", Sstate)])
            while mod_state["blk"] < 24:
                mod_block()
            stage("p2", [("oT", av(32 * K, KC * T, BF16)), ("Sstate", Sstate), ("modT", modT[:])])
            S.op("dve", lambda e: e.scalar_tensor_tensor(out=AB[:, 32:48], in0=modT[:, 64:80], scalar=1.0, in1=nwT[:, 16:32],
                                                         op0=ALU.add, op1=ALU.mult),
                 reads=("modT", "nwT", "AB"), writes=("AB",))
            S.op("dve", lambda e: e.tensor_copy(out=AB[:, 48:64], in_=modT[:, 48:64]), reads=("modT", "AB"), writes=("AB",))
            S.barrier()
            phase34()

        def phase1(slot, main):
            for t in range(NT):
                sl = t % 2
                S.dma("sp", f"xs{sl}", lambda e, t=t, sl=sl: e.dma_start(out=xst[sl], in_=x_d[slot, t * 128:(t + 1) * 128, :]),
                      writes=(f"xst{sl}",))
                for g4 in range(4):
                    bank = 4 + (g4 % 2)
                    for j in range(4):
                        kc = g4 * 4 + j
                        S.op("pe", lambda e, sl=sl, kc=kc, j=j, bank=bank: e.transpose(
                            ps[bank][:, j * 128:(j + 1) * 128], xst[sl][:, kc * 128:(kc + 1) * 128], ident[:]),
                            reads=(f"xst{sl}", "ident"), writes=(f"ps{bank}",), signal=(j == 3))
                    dst = xT[:, g4 * 4:(g4 + 1) * 4, t * 128:(t + 1) * 128]
                    src = ps[bank][:].rearrange("p (a b) -> p a b", b=128)
                    if g4 % 2 == 0:
                        S.op("act", lambda e, dst=dst, src=src: e.activation(out=dst, in_=src, func=AF.Copy),
                             reads=(f"ps{bank}",), writes=(f"xT{t}a",))
                    else:
                        S.op("dve", lambda e, dst=dst, src=src: e.tensor_copy(out=dst, in_=src),
                             reads=(f"ps{bank}",), writes=(f"xT{t}d",))
            xT_all = tuple(f"xT{t}{c}" for t in range(NT) for c in "ad")
            if main:
                for q in range(4):
                    S.dma("sp", "xscrw", lambda e, q=q: e.dma_start(out=xscr.ap()[:, q * 4096:(q + 1) * 4096],
                                                                   in_=xTf[:, q * 4096:(q + 1) * 4096]),
                          reads=xT_all, writes=("xscr",))
                S.settle("xscrw", ["xscr"])
            for kc in range(KC):
                sl = kc % 2
                S.op("act", lambda e, kc=kc, sl=sl: e.activation(out=xsq[sl], in_=xT[:, kc, :], func=AF.Square),
                     reads=xT_all, writes=(f"xsq{sl}",))
                for tb in range(2):
                    S.op("pe", lambda e, kc=kc, sl=sl, tb=tb: e.matmul(
                        ps[5 + tb][:, :], onesb[:], xsq[sl][:, tb * 512:(tb + 1) * 512],
                        start=(kc == 0), stop=(kc == KC - 1)),
                        reads=(f"xsq{sl}", "onesb"), writes=(f"ps{5 + tb}",), signal=True)
            for tb in range(2):
                rstd_from(ps[5 + tb][:, :], rstd[:, tb * 512:(tb + 1) * 512], float(D), (f"ps{5 + tb}",), f"rstd{tb}")
            for kc in range(KC):
                sl = kc % 2
                S.op("dve", lambda e, kc=kc, sl=sl: e.scalar_tensor_tensor(
                    out=tmpf[sl], in0=xT[:, kc, :], scalar=AB[:, kc:kc + 1], in1=rstd, op0=ALU.mult, op1=ALU.mult),
                    reads=xT_all + ("AB", "rstd0", "rstd1"), writes=(f"tmpf{sl}",))
                S.op("act", lambda e, kc=kc, sl=sl: e.activation(
                    out=hT[:, kc, :], in_=tmpf[sl], func=AF.Identity, bias=AB[:, 16 + kc:17 + kc], scale=1.0),
                    reads=(f"tmpf{sl}", "AB"), writes=("hT",))

        def lr_proj():
            wlr3 = wlr[:].rearrange("p (k n) -> p k n", n=16)
            for tb in range(2):
                for kc in range(KC):
                    S.op("pe", lambda e, kc=kc, tb=tb: e.matmul(
                        ps[6][0:16, :], wlr3[:, kc, :], hT[:, kc, tb * 512:(tb + 1) * 512],
                        start=(kc == 0), stop=(kc == KC - 1)),
                        reads=("wlr", "hT"), writes=("ps6",), signal=(kc == KC - 1))
                S.op("act", lambda e, tb=tb: e.activation(out=lrT[0:16, tb * 512:(tb + 1) * 512], in_=ps[6][0:16, :],
                                                          func=AF.Copy), reads=("ps6",), writes=("lrT",))

        def proj_fm(slot, ncols_blk, c0, evac):
            wv = ring[slot][:, 0:KC * ncols_blk].rearrange("p (k n) -> p k n", n=ncols_blk)
            for tb in range(2):
                b = nextbank()
                for kc in range(KC):
                    S.op("pe", lambda e, wv=wv, kc=kc, tb=tb, b=b: e.matmul(
                        ps[b][:, :], wv[:, kc, c0:c0 + 128], hT[:, kc, tb * 512:(tb + 1) * 512],
                        start=(kc == 0), stop=(kc == KC - 1)),
                        reads=(f"ring{slot}", "hT"), writes=(f"ps{b}",), signal=(kc == KC - 1))
                evac(tb, b)

        def proj_tm(slot, ncols_blk, c0, n, evac):
            wv = ring[slot][:, 0:KC * ncols_blk].rearrange("p (k n) -> p k n", n=ncols_blk)
            for t in range(NT):
                b = nextbank()
                for kc in range(KC):
                    S.op("pe", lambda e, wv=wv, kc=kc, t=t, b=b: e.matmul(
                        ps[b][:, 0:n], hT[:, kc, t * 128:(t + 1) * 128], wv[:, kc, c0:c0 + n],
                        start=(kc == 0), stop=(kc == KC - 1)),
                        reads=(f"ring{slot}", "hT"), writes=(f"ps{b}",), signal=(kc == KC - 1))
                evac(t, b)

        def unit(slot_i, main, is_gla, idx, nv):
            def ust(nm, aps):
                if slot_i == 0 and is_gla and idx == 0:
                    stage(nm, aps)
                if slot_i == 0 and (not is_gla) and idx == 0:
                    stage("h" + nm[1:], aps)
            Dv = nv * 128
            scol = idx * 256 if is_gla else 1024 + idx * 128
            bs = (-1.0 / 16.0) if is_gla else 1.0
            vt3 = vtok[:, 0:NT * Dv].rearrange("p (t e) -> p t e", e=Dv)
            Sb3 = Sbf[:, 0:16 * Dv].rearrange("p (c e) -> p c e", e=Dv)
            if is_gla:
                if main:
                    slotA = wload(wglaa_d[idx], KC * 512)
                    slotB = wload(wglav_d[idx], KC * 256)
                    nA, kc0, nB, vc0 = 512, 128, 256, 0
                else:
                    slotA = wload(wglap_d[idx], KC * 384)
                    slotB = slotA
                    nA, kc0, nB, vc0 = 384, 0, 384, 128
            else:
                if main:
                    slotA = wload(whg_d[idx], KC * 512)
                    nA, fc0 = 512, 256
                else:
                    slotA = wload(whgp_d[idx], KC * 256)
                    nA, fc0 = 256, 0
            if is_gla:
                for half in range(2):
                    b = nextbank()
                    for j in range(4):
                        t = half * 4 + j
                        S.op("pe", lambda e, t=t, j=j, b=b: e.matmul(
                            ps[b][:, j * 128:(j + 1) * 128], lrT[:, t * 128:(t + 1) * 128],
                            wg[:, idx * 128:(idx + 1) * 128], start=True, stop=True),
                            reads=("lrT", "wg"), writes=(f"ps{b}",), signal=(j == 3))
                    dst = graw[:, half * 4:(half + 1) * 4, :]
                    src = ps[b][:].rearrange("p (a b) -> p a b", b=128)
                    S.op("act", lambda e, dst=dst, src=src: e.activation(out=dst, in_=src, func=AF.Exp, scale=-1.0),
                         reads=(f"ps{b}",), writes=("graw",))
                S.op("act", lambda e: e.activation(out=graw, in_=graw, func=AF.Ln, bias=onet[:], scale=1.0),
                     reads=("graw", "onet"), writes=("graw",))
                if True:
                    def ev_v(t, b):
                        S.op("act", lambda e, t=t, b=b: e.activation(out=vt3[:, t, :], in_=ps[b][:, 0:256], func=AF.Copy),
                             reads=(f"ps{b}",), writes=("vtok",))
                    proj_tm(slotB, nB, vc0, 256, ev_v)
                if main:
                    for s_ in range(nv):
                        c0 = (256 + s_ * 128) if is_gla else 128

                        def ev_g(tb, b, s_=s_):
                            sl_ = slice(tb * 512, (tb + 1) * 512)
                            S.op("act", lambda e, b=b, sl_=sl_: e.activation(out=gate[:, s_, sl_], in_=ps[b][:, :], func=AF.Silu),
                                 reads=(f"ps{b}",), writes=("gate",))
                        proj_fm(slotA, nA, c0, ev_g)
            else:
                S.dma("sp", "lbl", lambda e: e.dma_start(out=lbt[:, 0:256], in_=lbl_d[idx]), writes=("lbt",))
                S.op("dve", lambda e: e.tensor_tensor(out=lbt[:, 512:640], in0=lbt[:, 128:256], in1=lbt[:, 0:128],
                                                      op=ALU.subtract), reads=("lbt",), writes=("lbt2",))
                S.op("act", lambda e: e.activation(out=lbt[:, 512:640], in_=lbt[:, 512:640], func=AF.Exp),
                     reads=("lbt2",), writes=("lbt2",))
                S.op("dve", lambda e: e.tensor_scalar(out=lbt[:, 256:384], in0=lbt[:, 512:640], scalar1=1.0, scalar2=None,
                                                      op0=ALU.add), reads=("lbt2",), writes=("lb",))
                S.op("dve", lambda e: e.reciprocal(out=lbt[:, 256:384], in_=lbt[:, 256:384]), reads=("lb",), writes=("lb",))
                S.op("dve", lambda e: e.tensor_tensor(out=lbt[:, 384:512], in0=lbt[:, 512:640], in1=lbt[:, 256:384],
                                                      op=ALU.mult), reads=("lb", "lbt2"), writes=("oml",))

                def ev_fi(t, b):
                    S.op("act", lambda e, t=t, b=b: e.activation(out=graw[:, t, :], in_=ps[b][:, 0:128], func=AF.Exp,
                                                                 scale=-1.0), reads=(f"ps{b}",), writes=("graw",))
                    S.op("act", lambda e, t=t, b=b: e.activation(out=vt3[:, t, :], in_=ps[b][:, 128:256], func=AF.Copy),
                         reads=(f"ps{b}",), writes=("vtok",))
                proj_tm(slotA, nA, fc0, 256, ev_fi)
                if main:
                    def ev_q0(tb, b):
                        sl_ = slice(tb * 512, (tb + 1) * 512)
                        S.op("act", lambda e, b=b, sl_=sl_: e.activation(out=qf[:, sl_], in_=ps[b][:, :], func=AF.Silu),
                             reads=(f"ps{b}",), writes=("qf",))
                    proj_fm(slotA, nA, 0, ev_q0)
                if main:
                    for s_ in range(nv):
                        c0 = (256 + s_ * 128) if is_gla else 128

                        def ev_g(tb, b, s_=s_):
                            sl_ = slice(tb * 512, (tb + 1) * 512)
                            S.op("act", lambda e, b=b, sl_=sl_: e.activation(out=gate[:, s_, sl_], in_=ps[b][:, :], func=AF.Silu),
                                 reads=(f"ps{b}",), writes=("gate",))
                        proj_fm(slotA, nA, c0, ev_g)
                S.op("dve", lambda e: e.tensor_scalar(out=graw, in0=graw, scalar1=1.0, scalar2=None, op0=ALU.add),
                     reads=("graw",), writes=("graw",))
                S.op("dve", lambda e: e.reciprocal(out=graw, in_=graw), reads=("graw",), writes=("graw",))
                S.op("dve", lambda e: e.tensor_tensor(out=graw, in0=graw,
                                                      in1=lbt[:, 384:512].unsqueeze(1).to_broadcast([128, NT, 128]),
                                                      op=ALU.mult), reads=("graw", "oml"), writes=("graw",))
                S.op("dve", lambda e: e.tensor_tensor(out=graw, in0=graw,
                                                      in1=lbt[:, 256:384].unsqueeze(1).to_broadcast([128, NT, 128]),
                                                      op=ALU.add), reads=("graw", "lb"), writes=("graw",))
                S.op("act", lambda e: e.activation(out=graw, in_=graw, func=AF.Ln), reads=("graw",), writes=("graw",))
            ust("ua", [("graw", av(TB0, NT * 128, F32))])
            ghi = ghl[:, 0:NT * 128].rearrange("p (t d) -> p t d", d=128)
            glo = ghl[:, NT * 128:2 * NT * 128].rearrange("p (t d) -> p t d", d=128)
            S.op("dve", lambda e: e.tensor_copy(out=ghi, in_=graw), reads=("graw",), writes=("ghi",))
            S.op("dve", lambda e: e.tensor_tensor(out=glo, in0=graw, in1=ghi, op=ALU.subtract), reads=("graw", "ghi"), writes=("glo",))
            for half in range(2):
                for j in range(4):
                    t = half * 4 + j
                    S.op("pe", lambda e, t=t, j=j, half=half: e.matmul(
                        ps[4 + half][:, j * 128:(j + 1) * 128], ghi[:, t, :], maskb[:], start=True, stop=False),
                        reads=("ghi", "maskb"), writes=(f"ps{4 + half}",), signal=False)
                    S.op("pe", lambda e, t=t, j=j, half=half: e.matmul(
                        ps[4 + half][:, j * 128:(j + 1) * 128], glo[:, t, :], maskb[:], start=False, stop=True),
                        reads=("glo", "maskb"), writes=(f"ps{4 + half}",), signal=(j == 3))
                sl_ = slice(half * 512, (half + 1) * 512)
                if main:
                    S.op("act", lambda e, half=half, sl_=sl_: e.activation(out=eb[:, sl_], in_=ps[4 + half][:, :], func=AF.Exp,
                                                                           scale=bs), reads=(f"ps{4 + half}",), writes=("eb",))
                S.op("act", lambda e, half=half, sl_=sl_: e.activation(out=enb[:, sl_], in_=ps[4 + half][:, :], func=AF.Exp,
                                                                       scale=-bs), reads=(f"ps{4 + half}",), writes=("enb",))
            ust("ub", [("enb", enb), ("small", small)])
            S.op("dve", lambda e: e.reciprocal(out=Ecol, in_=enb.rearrange("p (c j) -> p c j", j=64)[:, :, 63]),
                 reads=("enb",), writes=("Ecol",))
            if not is_gla:
                for half in range(2):
                    b = nextbank()
                    for j in range(4):
                        t = half * 4 + j
                        S.op("pe", lambda e, t=t, j=j, b=b: e.matmul(
                            ps[b][:, j * 128:(j + 1) * 128], ghi[:, t, :], identb[:], start=True, stop=False),
                            reads=("ghi", "identb"), writes=(f"ps{b}",), signal=False)
                        S.op("pe", lambda e, t=t, j=j, b=b: e.matmul(
                            ps[b][:, j * 128:(j + 1) * 128], glo[:, t, :], identb[:], start=False, stop=True),
                            reads=("glo", "identb"), writes=(f"ps{b}",), signal=(j == 3))
                    sl_ = slice(half * 512, (half + 1) * 512)
                    S.op("act", lambda e, b=b, sl_=sl_: e.activation(out=khf[:, sl_], in_=ps[b][:, :], func=AF.Exp),
                         reads=(f"ps{b}",), writes=("khf",))
                S.op("dve", lambda e: e.tensor_scalar(out=khf, in0=khf, scalar1=-1.0, scalar2=1.0, op0=ALU.mult, op1=ALU.add),
                     reads=("khf",), writes=("khf",))
                S.op("dve", lambda e: e.tensor_tensor(out=khf, in0=khf, in1=enb, op=ALU.mult),
                     reads=("khf", "enb"), writes=("khf",))
            qscale = 128.0 ** -0.5 if is_gla else 1.0
            if is_gla:
                if main:
                    def ev_q(tb, b):
                        sl_ = slice(tb * 512, (tb + 1) * 512)
                        S.op("dve", lambda e, b=b, sl_=sl_: e.scalar_tensor_tensor(
                            out=qt[:, sl_], in0=ps[b][:, :], scalar=qscale, in1=eb[:, sl_], op0=ALU.mult, op1=ALU.mult),
                            reads=(f"ps{b}", "eb"), writes=("qt",))
                    proj_fm(slotA, nA, 0, ev_q)

                def ev_k(tb, b):
                    sl_ = slice(tb * 512, (tb + 1) * 512)
                    S.op("dve", lambda e, b=b, sl_=sl_: e.tensor_tensor(out=khf[:, sl_], in0=ps[b][:, :], in1=enb[:, sl_],
                                                                        op=ALU.mult), reads=(f"ps{b}", "enb"), writes=("khf",))
                proj_fm(slotA, nA, kc0, ev_k)
            elif main:
                S.op("dve", lambda e: e.tensor_tensor(out=qt, in0=qf, in1=eb, op=ALU.mult),
                     reads=("qf", "eb"), writes=("qt",))
            ust("uc", [("khf", khf), ("small", small)])
            ust("ud", [("vtok", vtok)])
            if main:
                S.op("dve", lambda e: e.tensor_copy(out=kh, in_=khf), reads=("khf",), writes=("kh",))
            S.op("dve", lambda e: e.tensor_tensor(out=kt.rearrange("p (c j) -> p c j", j=64),
                                                  in0=khf.rearrange("p (c j) -> p c j", j=64),
                                                  in1=Ecol.unsqueeze(2).to_broadcast([128, 16, 64]), op=ALU.mult),
                 reads=("khf", "Ecol"), writes=("kt",))
            for t in range(NT):
                S.op("pe", lambda e, t=t: e.transpose(pst[:, t * 128:(t + 1) * 128], kt[:, t * 128:(t + 1) * 128], identb[:]),
                     reads=("kt", "identb"), writes=("pst",), signal=(t == NT - 1))
            S.op("act", lambda e: e.activation(out=ktok, in_=pst[:].rearrange("p (t d) -> p t d", d=128), func=AF.Copy),
                 reads=("pst",), writes=("ktok",))
            ust("ue", [("ktok", av(TB0 + 26 * K, NT * 128, BF16)), ("kt", kt)])
            if main:
                for half in range(2):
                    for j in range(4):
                        t = half * 4 + j
                        S.op("pe", lambda e, t=t, j=j, half=half: e.matmul(
                            ps[4 + half][:, j * 128:(j + 1) * 128], kh[:, t * 128:(t + 1) * 128], qt[:, t * 128:(t + 1) * 128],
                            start=True, stop=True), reads=("kh", "qt"), writes=(f"ps{4 + half}",), signal=(j == 3))
                    S.op("dve", lambda e, half=half: e.tensor_tensor(
                        out=sT[:, half * 4:(half + 1) * 4, :], in0=ps[4 + half][:].rearrange("p (a b) -> p a b", b=128),
                        in1=maskf[:].unsqueeze(1).to_broadcast([128, 4, 128]), op=ALU.mult),
                        reads=(f"ps{4 + half}", "maskf"), writes=("sT",))
            S.op("dve", lambda e: e.tensor_scalar(out=Sf[0][:, 0:Dv], in0=Sstate[:, scol:scol + Dv],
                                                  scalar1=keep[:, slot_i:slot_i + 1], scalar2=None, op0=ALU.mult),
                 reads=("Sstate", "keep"), writes=("Sf0",))
            if main:
                S.op("dve", lambda e: e.tensor_copy(out=Sb3[:, 0, :], in_=Sf[0][:, 0:Dv]), reads=("Sf0",), writes=("Sbf",))
            cpb = 512 // Dv
            for c0_ in range(0, NCH, 2 * cpb):
                bx, by = nextbank(), nextbank()
                loc = {}
                for j in range(2 * cpb):
                    c = c0_ + j
                    t, r0 = c // 2, (c % 2) * 64
                    bb = bx if c % 2 == 0 else by
                    col = (j // 2) * Dv
                    loc[c] = (bb, col)
                    S.op("pe", lambda e, t=t, r0=r0, bb=bb, col=col: e.matmul(
                        ps[bb][:, col:col + Dv], ktok[r0:r0 + 64, t, :], vt3[r0:r0 + 64, t, :], start=True, stop=True),
                        reads=("ktok", "vtok"), writes=(f"ps{bx}", f"ps{by}"), signal=(j == 2 * cpb - 1))
                for j in range(2 * cpb):
                    c = c0_ + j
                    bb, col = loc[c]
                    a_, b_ = Sf[c % 2], Sf[(c + 1) % 2]
                    if main and c < NCH - 1:
                        S.op("dve", lambda e, c=c, bb=bb, col=col, a_=a_: e.scalar_tensor_tensor(
                            out=Sb3[:, c + 1, :], in0=a_[:, 0:Dv], scalar=Ecol[:, c:c + 1], in1=ps[bb][:, col:col + Dv],
                            op0=ALU.mult, op1=ALU.add), reads=(f"Sf{c % 2}", "Ecol", f"ps{bb}"), writes=("Sbf",))
                    S.op("dve", lambda e, c=c, bb=bb, col=col, a_=a_, b_=b_: e.scalar_tensor_tensor(
                        out=b_[:, 0:Dv], in0=a_[:, 0:Dv], scalar=Ecol[:, c:c + 1], in1=ps[bb][:, col:col + Dv],
                        op0=ALU.mult, op1=ALU.add), reads=(f"Sf{c % 2}", "Ecol", f"ps{bb}"), writes=(f"Sf{(c + 1) % 2}",))
            S.op("dve", lambda e: e.tensor_copy(out=Sstate[:, scol:scol + Dv], in_=Sf[0][:, 0:Dv]),
                 reads=("Sf0", "Sstate"), writes=("Sstate",))
            ust("uf", [("Sstate", Sstate)])
            if not main:
                return
            for s_ in range(nv):
                bx, by = nextbank(), nextbank()
                for t in range(NT):
                    for cb in range(2):
                        r0 = cb * 64
                        c = 2 * t + cb
                        bb = bx if cb == 0 else by
                        o_ap = ps[bb][:, t * 64:(t + 1) * 64]
                        S.op("pe", lambda e, t=t, r0=r0, o_ap=o_ap, s_=s_: e.matmul(
                            o_ap, vt3[r0:r0 + 64, t, s_ * 128:(s_ + 1) * 128], sT[r0:r0 + 64, t, r0:r0 + 64],
                            start=True, stop=False), reads=("vtok", "sT"), writes=(f"ps{bx}", f"ps{by}"), signal=False)
                        S.op("pe", lambda e, t=t, r0=r0, o_ap=o_ap, c=c, s_=s_: e.matmul(
                            o_ap, Sb3[:, c, s_ * 128:(s_ + 1) * 128], qt[:, t * 128 + r0: t * 128 + r0 + 64],
                            start=False, stop=True), reads=("Sbf", "qt"), writes=(f"ps{bx}", f"ps{by}"),
                            signal=(t == NT - 1 and cb == 1))
                ol4 = oloc[:, s_, :].rearrange("p (t c j) -> p t c j", c=2, j=64)
                S.op("act", lambda e, bx=bx, ol4=ol4: e.activation(
                    out=ol4[:, :, 0, :], in_=ps[bx][:].rearrange("p (t j) -> p t j", j=64), func=AF.Copy),
                    reads=(f"ps{bx}",), writes=("oloc",))
                S.op("act", lambda e, by=by, ol4=ol4: e.activation(
                    out=ol4[:, :, 1, :], in_=ps[by][:].rearrange("p (t j) -> p t j", j=64), func=AF.Copy),
                    reads=(f"ps{by}", "oloc"), writes=("oloc",))
            stage("u%d" % (idx if is_gla else 4 + idx), [("oloc", av(121 * K, 2 * T, F32)), ("qt", qt), ("eb", eb),
                  ("enb", enb), ("khf", khf), ("graw", av(TB0, NT * 128, F32)), ("vtok", vtok), ("Sstate", Sstate),
                  ("small", small), ("lrT", lrT[:]), ("Sbf", Sbf)])

        def finish(nv, et0, nwt):
            for s_ in range(nv):
                S.op("act", lambda e, s_=s_: e.activation(out=osq, in_=oloc[:, s_, :], func=AF.Square),
                     reads=("oloc",), writes=("osq",))
                for tb in range(2):
                    S.op("pe", lambda e, tb=tb, s_=s_: e.matmul(
                        ps[5 + tb][:, :], onesb[:], osq[:, tb * 512:(tb + 1) * 512], start=(s_ == 0), stop=(s_ == nv - 1)),
                        reads=("osq", "onesb"), writes=(f"ps{5 + tb}",))
            for tb in range(2):
                rstd_from(ps[5 + tb][:, :], crstd[:, tb * 512:(tb + 1) * 512], float(nv * 128), (f"ps{5 + tb}",), f"crstd{tb}")
            for s_ in range(nv):
                S.op("dve", lambda e, s_=s_: e.scalar_tensor_tensor(
                    out=ctmp, in0=oloc[:, s_, :], scalar=nwt[:, s_:s_ + 1], in1=crstd, op0=ALU.mult, op1=ALU.mult),
                    reads=("oloc", "crstd0", "crstd1", "glanw", "hgnw"), writes=("ctmp",))
                S.op("dve", lambda e, s_=s_: e.tensor_tensor(out=oT[:, et0 + s_, :], in0=ctmp, in1=gate[:, s_, :],
                                                             op=ALU.mult), reads=("ctmp", "gate"), writes=("oT",))

        def phase34():
            x1T = av(64 * K, KC * T, F32).rearrange("p (k t) -> p k t", t=T)
            x1f = av(64 * K, KC * T, F32)
            sq2 = [av(128 * K + i * K, 512, BF16) for i in range(4)]
            for q in range(4):
                S.dma("sp", "xscrr", lambda e, q=q: e.dma_start(out=x1f[:, q * 4096:(q + 1) * 4096],
                                                               in_=xscr.ap()[:, q * 4096:(q + 1) * 4096]),
                      reads=("xscr",), writes=("x1T",))
            S.settle("xscrr", ["x1T"])
            sqn = {"n": 0}

            def resid_evac(b, dct, tb, gcol, ssbank, first, last):
                sl_ = slice(tb * 512, (tb + 1) * 512)
                S.op("dve", lambda e: e.scalar_tensor_tensor(
                    out=x1T[:, dct, sl_], in0=ps[b][:, :], scalar=modT[:, gcol:gcol + 1], in1=x1T[:, dct, sl_],
                    op0=ALU.mult, op1=ALU.add), reads=(f"ps{b}", "modT", "x1T"), writes=("x1T",))
                i = sqn["n"] % 4
                sqn["n"] += 1
                S.op("act", lambda e, i=i: e.activation(out=sq2[i], in_=x1T[:, dct, sl_], func=AF.Square),
                     reads=("x1T",), writes=(f"sq2{i}",))
                S.op("pe", lambda e, i=i: e.matmul(ps[ssbank][:, :], onesb[:], sq2[i], start=first, stop=last),
                     reads=(f"sq2{i}", "onesb"), writes=(f"ps{ssbank}",))

            for wb in range(8):
                slot = wload(wout_d[wb], KC * 256)
                wv = ring[slot][:, 0:KC * 256].rearrange("p (k n) -> p k n", n=256)
                for ct in range(2):
                    dct = wb * 2 + ct
                    for tb in range(2):
                        b = nextbank()
                        for kc in range(KC):
                            S.op("pe", lambda e, wv=wv, kc=kc, ct=ct, tb=tb, b=b: e.matmul(
                                ps[b][:, :], wv[:, kc, ct * 128:(ct + 1) * 128], oT[:, kc, tb * 512:(tb + 1) * 512],
                                start=(kc == 0), stop=(kc == KC - 1)),
                                reads=(f"ring{slot}", "oT"), writes=(f"ps{b}",), signal=(kc == KC - 1))
                        resid_evac(b, dct, tb, 32 + dct, 5 + tb, dct == 0, dct == KC - 1)
            stage("p3", [("x1T", x1f)])

            hid = av(0, HKC * 512, BF16).rearrange("p (k t) -> p k t", t=512)
            h2T = av(44 * K, KC * 512, BF16).rearrange("p (k t) -> p k t", t=512)
            rstd2 = av(60 * K, 512, F32)
            tmp2 = av(62 * K, 512, F32)
            sa = [av(132 * K + i * 2 * K, 512, F32) for i in range(2)]
            ost = [av(136 * K + i * 4 * K, 1024, F32) for i in range(2)]
            for tb in range(2):
                sl_ = slice(tb * 512, (tb + 1) * 512)
                rstd_from(ps[5 + tb][:, :], rstd2, float(D), (f"ps{5 + tb}",), "rstd2")
                for kc in range(KC):
                    S.op("dve", lambda e, kc=kc, sl_=sl_: e.scalar_tensor_tensor(
                        out=tmp2, in0=x1T[:, kc, sl_], scalar=AB[:, 32 + kc:33 + kc], in1=rstd2, op0=ALU.mult, op1=ALU.mult),
                        reads=("x1T", "AB", "rstd2"), writes=("tmp2",))
                    S.op("act", lambda e, kc=kc: e.activation(out=h2T[:, kc, :], in_=tmp2, func=AF.Identity,
                                                              bias=AB[:, 48 + kc:49 + kc], scale=1.0),
                         reads=("tmp2", "AB"), writes=("h2T",))
                for fb in range(22):
                    slot = wload(wf1_d[fb], KC * 512)
                    wv = ring[slot][:, 0:KC * 512].rearrange("p (k n) -> p k n", n=512)
                    for hh in range(2):
                        ht = fb * 2 + hh
                        ba, bu = nextbank(), nextbank()
                        for (bb, c0) in ((ba, hh * 128), (bu, 256 + hh * 128)):
                            for kc in range(KC):
                                S.op("pe", lambda e, wv=wv, kc=kc, bb=bb, c0=c0: e.matmul(
                                    ps[bb][:, :], wv[:, kc, c0:c0 + 128], h2T[:, kc, :], start=(kc == 0), stop=(kc == KC - 1)),
                                    reads=(f"ring{slot}", "h2T"), writes=(f"ps{bb}",), signal=(kc == KC - 1))
                        i = ht % 2
                        S.op("act", lambda e, ba=ba, i=i: e.activation(out=sa[i], in_=ps[ba][:, :], func=AF.Silu),
                             reads=(f"ps{ba}",), writes=(f"sa{i}",))
                        S.op("dve", lambda e, bu=bu, i=i, ht=ht: e.tensor_tensor(out=hid[:, ht, :], in0=ps[bu][:, :], in1=sa[i],
                                                                                op=ALU.mult), reads=(f"ps{bu}", f"sa{i}"), writes=("hid",))
                for dct in range(KC):
                    slot = wload(wf2_d[dct], HKC * 128)
                    wv = ring[slot][:, 0:HKC * 128].rearrange("p (k n) -> p k n", n=128)
                    b = nextbank()
                    for kc in range(HKC):
                        S.op("pe", lambda e, wv=wv, kc=kc, b=b: e.matmul(
                            ps[b][:, :], wv[:, kc, :], hid[:, kc, :], start=(kc == 0), stop=(kc == HKC - 1)),
                            reads=(f"ring{slot}", "hid"), writes=(f"ps{b}",), signal=(kc == HKC - 1))
                    resid_evac(b, dct, tb, 80 + dct, 4, dct == 0, dct == KC - 1)
                rstd_from(ps[4][:, :], rstd2, float(D), ("ps4",), "rstd2")
                for kc in range(KC):
                    S.op("dve", lambda e, kc=kc, sl_=sl_: e.scalar_tensor_tensor(
                        out=x1T[:, kc, sl_], in0=x1T[:, kc, sl_], scalar=nwT[:, 32 + kc:33 + kc], in1=rstd2,
                        op0=ALU.mult, op1=ALU.mult), reads=("x1T", "nwT", "rstd2"), writes=("x1T",))
                for tt in range(4):
                    t = tb * 4 + tt
                    for hf in range(2):
                        i = hf
                        for q in range(2):
                            b = nextbank()
                            for j in range(4):
                                kc = hf * 8 + q * 4 + j
                                S.op("pe", lambda e, kc=kc, j=j, b=b, t=t: e.transpose(
                                    ps[b][:, j * 128:(j + 1) * 128], x1T[:, kc, t * 128:(t + 1) * 128], ident[:]),
                                    reads=("x1T", "ident"), writes=(f"ps{b}",), signal=(j == 3))
                            if q == 0:
                                S.op("act", lambda e, b=b, i=i: e.activation(out=ost[i][:, 0:512], in_=ps[b][:, :], func=AF.Copy),
                                     reads=(f"ps{b}",), writes=(f"ost{i}",))
                            else:
                                S.op("dve", lambda e, b=b, i=i: e.tensor_copy(out=ost[i][:, 512:1024], in_=ps[b][:, :]),
                                     reads=(f"ps{b}",), writes=(f"ost{i}",))
                        S.dma("sp", f"ost{i}", lambda e, t=t, hf=hf, i=i: e.dma_start(
                            out=out_d[t * 128:(t + 1) * 128, hf * 1024:(hf + 1) * 1024], in_=ost[i]),
                            reads=(f"ost{i}",), writes=("outd",))

        try:
            emit()
        except _Stop:
            pass
        S.barrier()

        with nc.Block() as block:
            @block.tensor
            def _(eng):
                S.replay("pe", eng)

            @block.scalar
            def _(eng):
                S.replay("act", eng)

            @block.vector
            def _(eng):
                S.replay("dve", eng)

            @block.gpsimd
            def _(eng):
                S.replay("pool", eng)

            @block.sync
            def _(eng):
                S.replay("sp", eng)
    return nc


def _prep_inputs(x, c, w_ada, b_ada, norm_mix_w, w_in, w_gla_gate, b_gla_gate, gla_norm_w,
                 hg_lower_bound_logits, hg_norm_w, w_out, norm_ffn_w, w_ffn_in, w_ffn_out, final_norm_w):
    f = lambda a: np.ascontiguousarray(np.asarray(a, dtype=np.float32))
    x, c, w_ada, b_ada, w_in = f(x), f(c), f(w_ada)[0], f(b_ada)[0], f(w_in)[0]
    w_gla_gate, b_gla_gate = f(w_gla_gate)[0], f(b_gla_gate)[0]
    w_out, w_ffn_in, w_ffn_out = f(w_out)[0], f(w_ffn_in)[0], f(w_ffn_out)[0]
    lbl = f(hg_lower_bound_logits)

    def pk(v, n):
        return np.ascontiguousarray(v.reshape(n, 128).T)

    def blk(w, cols):
        kc = w.shape[0] // 128
        return np.ascontiguousarray(w[:, cols].reshape(kc, 128, len(cols)).transpose(1, 0, 2).reshape(128, -1))

    ar = np.arange
    shared = {}
    shared["b_adaT"] = pk(b_ada, 96)
    shared["nwT"] = np.ascontiguousarray(np.concatenate(
        [pk(f(norm_mix_w)[0], 16), pk(f(norm_ffn_w)[0], 16), pk(f(final_norm_w), 16)], axis=1))
    shared["w_gla_a"] = np.stack([blk(w_in, np.concatenate([ar(h * 128, (h + 1) * 128), 512 + ar(h * 128, (h + 1) * 128),
                                                            2048 + ar(h * 256, (h + 1) * 256)])) for h in range(4)])
    shared["w_gla_v"] = np.stack([blk(w_in, 1024 + ar(h * 256, (h + 1) * 256)) for h in range(4)])
    shared["w_hg_p"] = np.stack([blk(w_in, np.concatenate([4112 + ar(j * 128, (j + 1) * 128), 5136 + ar(j * 128, (j + 1) * 128)]))
                                 for j in range(8)])
    shared["w_gla_p"] = np.stack([blk(w_in, np.concatenate([512 + ar(h * 128, (h + 1) * 128), 1024 + ar(h * 256, (h + 1) * 256)]))
                                  for h in range(4)])
    shared["w_ada_b"] = np.stack([blk(w_ada, ar(q * 512, (q + 1) * 512)) for q in range(24)])
    hg = []
    for j in range(8):
        cols = np.concatenate([3088 + ar(j * 128, (j + 1) * 128), 6160 + ar(j * 128, (j + 1) * 128),
                               4112 + ar(j * 128, (j + 1) * 128), 5136 + ar(j * 128, (j + 1) * 128)])
        hg.append(blk(w_in, cols))
    shared["w_hg_u"] = np.stack(hg)
    shared["w_lr"] = blk(w_in, ar(3072, 3088))
    shared["wg_aug"] = np.ascontiguousarray(np.concatenate([w_gla_gate, b_gla_gate[None, :]], axis=0))
    shared["gla_nwT"] = pk(f(gla_norm_w)[0], 2)
    shared["hg_nwT"] = pk(f(hg_norm_w)[0], 1)
    lb = np.stack([np.concatenate([lbl[0, j * 128:(j + 1) * 128], lbl[1, j * 128:(j + 1) * 128]]) for j in range(8)])
    shared["lbl"] = np.ascontiguousarray(np.broadcast_to(lb[:, None, :], (8, 128, 256)))
    shared["w_out_b"] = np.stack([blk(w_out, ar(b * 256, (b + 1) * 256)) for b in range(8)])
    shared["w_f1"] = np.stack([blk(w_ffn_in, np.concatenate([ar(b * 256, (b + 1) * 256), FH + ar(b * 256, (b + 1) * 256)]))
                               for b in range(22)])
    shared["w_f2"] = np.stack([blk(w_ffn_out, ar(b * 128, (b + 1) * 128)) for b in range(16)])
    shared["ident"] = np.eye(128, dtype=np.float32)
    j = ar(128)[:, None]
    i = ar(128)[None, :]
    shared["maskT"] = ((j // 64 == i // 64) & (j <= i)).astype(np.float32)
    in_maps = []
    for r in range(NCORES):
        b, s = r // 4, r % 4
        m = dict(shared)
        segs = [max(j - (3 - s), 0) for j in range(3)] + [s]
        m["xs"] = np.ascontiguousarray(np.stack([x[b, q * T:(q + 1) * T, :] for q in segs]))
        m["cT"] = pk(c[b], 16)
        kp = np.ones((128, NSLOT), np.float32)
        kp[:, 3 - s] = 0.0
        m["keep"] = kp
        in_maps.append(m)
    return in_maps


_NC_CACHE = {}


def kernel(**inputs):
    in_maps = _prep_inputs(**inputs)
    if "nc" not in _NC_CACHE:
        _NC_CACHE["nc"] = build()
    res = run_bass_kernel_spmd(_NC_CACHE["nc"], in_maps, core_ids=list(range(NCORES)))
    out = np.empty((2, 4096, D), np.float32)
    for r in range(NCORES):
        b, s = r // 4, r % 4
        out[b, s * T:(s + 1) * T, :] = res.results[r]["out"]
    return out
```

```python
import numpy as np
from contextlib import ExitStack
import concourse.bass as bass
import concourse.mybir as mybir
from concourse.bass_utils import run_bass_kernel_spmd

F32 = mybir.dt.float32
BF16 = mybir.dt.bfloat16
AF = mybir.ActivationFunctionType
ALU = mybir.AluOpType

NCORES = 8
T = 1024
D = 2048
KC = 16
NT = 8
NCH = 16
FH = 5632
HKC = 44
EPS = 1e-6
ARENA = 144 * 1024


class Sched:
    ENG = ("pe", "act", "dve", "pool", "sp")

    def __init__(self, nc, es):
        self.nc = nc
        self.es = es
        self.sem = {k: es.enter_context(nc.semaphore("s_" + k)) for k in self.ENG}
        self.cnt = {k: 0 for k in self.ENG}
        self.seen = {k: {} for k in self.ENG}
        self.prog = {k: [] for k in self.ENG}
        self.lastw = {}
        self.readers = {}
        self.dsem = {}
        self.dcnt = {}

    def semobj(self, k):
        return self.dsem[k] if k in self.dsem else self.sem[k]

    def _waits(self, e, reads, writes):
        need = {}

        def add(k, v):
            if k == e and e == "pe":
                return
            if need.get(k, 0) < v:
                need[k] = v

        for r in reads:
            t = self.lastw.get(r)
            if t:
                add(*t)
        for w in writes:
            t = self.lastw.get(w)
            if t:
                add(*t)
            for k, v in self.readers.get(w, {}).items():
                add(k, v)
        out = []
        for k, v in need.items():
            if self.seen[e].get(k, 0) < v:
                self.seen[e][k] = v
                out.append((k, v))
        return out

    def _record(self, tok, reads, writes):
        k, v = tok
        for r in reads:
            d = self.readers.setdefault(r, {})
            if d.get(k, 0) < v:
                d[k] = v
        for w in writes:
            self.lastw[w] = tok
            self.readers[w] = {}

    def op(self, e, fn, reads=(), writes=(), signal=True):
        waits = self._waits(e, reads, writes)
        if signal:
            self.cnt[e] += 1
            tok = (e, self.cnt[e])
            inc = (e, 1)
        else:
            tok = (e, self.cnt[e] + 1)
            inc = None
        self.prog[e].append((waits, fn, inc))
        self._record(tok, reads, writes)

    def dma(self, q, dkey, fn, reads=(), writes=(), amount=16):
        waits = self._waits(q, reads, writes)
        if dkey not in self.dsem:
            self.dsem[dkey] = self.es.enter_context(self.nc.semaphore("d_" + dkey))
            self.dcnt[dkey] = 0
        self.dcnt[dkey] += (16 if amount == "cc" else amount)
        if amount == "cc":
            self.dcnt[dkey] -= 15
        tok = (dkey, self.dcnt[dkey])
        self.prog[q].append((waits, fn, (dkey, amount)))
        self._record(tok, reads, writes)
        return tok

    def settle(self, dkey, resources):
        tok = (dkey, self.dcnt[dkey])
        for r in resources:
            self.lastw[r] = tok

    def barrier(self):
        targets = [(k, self.cnt[k]) for k in self.ENG if self.cnt[k] > 0]
        targets += [(k, v) for k, v in self.dcnt.items() if v > 0]
        for e in self.ENG:
            waits = []
            for k, v in targets:
                if k == e and e in ("pe",):
                    continue
                if self.seen[e].get(k, 0) < v:
                    self.seen[e][k] = v
                    waits.append((k, v))
            if waits:
                self.prog[e].append((waits, None, None))

    def replay(self, e, eng):
        for waits, fn, inc in self.prog[e]:
            for k, v in waits:
                eng.wait_ge(self.semobj(k), v)
            if fn is None:
                continue
            ins = fn(eng)
            if inc:
                k, a = inc
                if a == "cc":
                    ins.then_inc(self.semobj(k))
                else:
                    ins.then_inc(self.semobj(k), a)


class _Stop(Exception):
    pass


NSLOT = 4


def build(stop_after=None):
    nc = bass.Bass("TRN2", target_bir_lowering=False)

    def din(name, shape):
        return nc.dram_tensor(name, list(shape), F32, kind="ExternalInput").ap()

    x_d = din("xs", [NSLOT, T, D])
    cT_d = din("cT", [128, KC])
    wada_d = din("w_ada_b", [24, 128, KC * 512])
    bada_d = din("b_adaT", [128, 96])
    nw_d = din("nwT", [128, 48])
    wglaa_d = din("w_gla_a", [4, 128, KC * 512])
    wglav_d = din("w_gla_v", [4, 128, KC * 256])
    whgp_d = din("w_hg_p", [8, 128, KC * 256])
    wglap_d = din("w_gla_p", [4, 128, KC * 384])
    whg_d = din("w_hg_u", [8, 128, KC * 512])
    wlr_d = din("w_lr", [128, KC * 16])
    wg_d = din("wg_aug", [17, 512])
    glanw_d = din("gla_nwT", [128, 2])
    hgnw_d = din("hg_nwT", [128, 1])
    lbl_d = din("lbl", [8, 128, 256])
    wout_d = din("w_out_b", [8, 128, KC * 256])
    wf1_d = din("w_f1", [22, 128, KC * 512])
    wf2_d = din("w_f2", [16, 128, HKC * 128])
    ident_d = din("ident", [128, 128])
    mask_d = din("maskT", [128, 128])
    keep_d = din("keep", [128, NSLOT])
    out_d = nc.dram_tensor("out", [T, D], F32, kind="ExternalOutput").ap()
    xscr = nc.dram_tensor("xscr", [128, KC * T], F32)

    with ExitStack() as es:
        def sb(name, shape, dt):
            return es.enter_context(nc.sbuf_tensor(name, list(shape), dt))

        ident = sb("ident_s", [128, 128], F32)
        maskf = sb("maskf", [128, 128], F32)
        identb = sb("identb", [128, 128], BF16)
        maskb = sb("maskb", [128, 128], BF16)
        ghl = sb("ghl", [128, 2 * NT * 128], BF16)
        onesb = sb("onesb", [128, 128], BF16)
        modT = sb("modT", [128, 96], F32)
        badaT = sb("badaT", [128, 96], F32)
        nwT = sb("nwT_s", [128, 48], F32)
        AB = sb("AB", [128, 64], F32)
        cT = sb("cT_s", [128, KC], F32)
        cact = sb("cact", [128, KC], BF16)
        keep = sb("keep_s", [128, NSLOT], F32)
        glanw = sb("glanw", [128, 2], F32)
        hgnw = sb("hgnw", [128, 1], F32)
        epst = sb("epst", [128, 1], F32)
        onet = sb("onet", [128, 1], F32)
        wg = sb("wg_s", [17, 512], F32)
        lrT = sb("lrT", [17, T], F32)
        wlr = sb("wlr_s", [128, KC * 16], BF16)
        ring = [sb(f"ring{i}", [128, 8192], BF16) for i in range(3)]
        arena = sb("arena", [128, ARENA // 2], BF16)
        ps = [es.enter_context(nc.psum_tensor(f"ps{i}", [128, 512], F32)) for i in range(7)]
        pst = es.enter_context(nc.psum_tensor("pst", [128, 1024], BF16))

        S = Sched(nc, es)

        def stage(name, aps):
            if stop_after != name:
                return
            S.barrier()
            for dname, ap in aps:
                d = nc.dram_tensor("dbg_" + dname, list(ap.shape), ap.dtype, kind="ExternalOutput").ap()
                S.dma("sp", "dbg", lambda e, d=d, ap=ap: e.dma_start(out=d, in_=ap), writes=("dbgo_" + dname,))
            raise _Stop()

        def av(off, nelem, dt):
            if dt is F32:
                return arena[:, off // 2: off // 2 + nelem * 2].bitcast(F32)
            return arena[:, off // 2: off // 2 + nelem]

        K = 1024
        hT = av(0, KC * T, BF16).rearrange("p (k t) -> p k t", t=T)
        xsq = [av(32 * K + i * 2 * K, T, BF16) for i in range(2)]
        tmpf = [av(36 * K + i * 4 * K, T, F32) for i in range(2)]
        rstd = av(44 * K, T, F32)
        xst = [av(48 * K + i * 8 * K, D, F32) for i in range(2)]
        xT = av(64 * K, KC * T, F32).rearrange("p (k t) -> p k t", t=T)
        xTf = av(64 * K, KC * T, F32)
        oT = av(32 * K, KC * T, BF16).rearrange("p (k t) -> p k t", t=T)
        TB0 = 64 * K
        o_ = TB0
        graw = av(o_, NT * 128, F32).rearrange("p (t d) -> p t d", d=128); o_ += 4 * K
        eb = av(o_, T, F32); o_ += 4 * K
        enb = av(o_, T, F32); o_ += 4 * K
        khf = av(o_, T, F32); o_ += 4 * K
        qf = av(o_, T, F32); o_ += 4 * K
        qt = av(o_, T, BF16); o_ += 2 * K
        kh = av(o_, T, BF16); o_ += 2 * K
        kt = av(o_, T, BF16); o_ += 2 * K
        ktok = av(o_, NT * 128, BF16).rearrange("p (t d) -> p t d", d=128); o_ += 2 * K
        vtok = av(o_, NT * 256, BF16); o_ += 4 * K
        sT = av(o_, NT * 128, BF16).rearrange("p (t i) -> p t i", i=128); o_ += 2 * K
        Sbf = av(o_, 16 * 256, BF16); o_ += 8 * K
        Sf = [av(o_ + i * K, 256, F32) for i in range(2)]; o_ += 2 * K
        small = av(o_, 128, F32); o_ += K // 2
        lbt = av(o_, 5 * 128, F32); o_ += 5 * K // 2
        osq = av(o_, T, BF16); o_ += 2 * K
        crstd = av(o_, T, F32); o_ += 4 * K
        ctmp = av(o_, T, F32); o_ += 4 * K
        assert o_ == 121 * K, o_
        oloc = av(o_, 2 * T, F32).rearrange("p (s t) -> p s t", t=T); o_ += 8 * K
        gate = av(o_, 2 * T, BF16).rearrange("p (s t) -> p s t", t=T); o_ += 4 * K
        assert o_ <= 135 * K, o_
        Sstate = av(135 * K, 2048, F32)
        Ecol = small[:, 0:16]
        bl = [small[:, 16:32], small[:, 32:48]]

        ring_state = {"n": 0}

        def wload(src_ap, ncols):
            i = ring_state["n"] % 3
            ring_state["n"] += 1
            S.dma("pool", f"ring{i}",
                  lambda e, i=i, src_ap=src_ap, ncols=ncols: e.dma_start(
                      out=(ring[i][:, 0:ncols].rearrange("p (k n) -> p k n", n=src_ap.shape[2])
                           if len(src_ap.shape) == 3 else ring[i][:, 0:ncols]), in_=src_ap),
                  reads=(), writes=(f"ring{i}",))
            return i

        abank = {"n": 0}

        def nextbank():
            b = abank["n"] % 4
            abank["n"] += 1
            return b

        def rstd_from(ps_ap, out_ap, dim, rkeys, wkey):
            S.op("act", lambda e: e.activation(out=out_ap, in_=ps_ap, func=AF.Ln, bias=epst[:], scale=1.0 / dim),
                 reads=tuple(rkeys) + ("epst",), writes=(wkey,))
            S.op("act", lambda e: e.activation(out=out_ap, in_=out_ap, func=AF.Exp, scale=-0.5),
                 reads=(wkey,), writes=(wkey,))

        mod_state = {"blk": 0}

        def mod_block():
            blk = mod_state["blk"]
            if blk >= 24:
                return
            mod_state["blk"] += 1
            slot = wload(wada_d[blk], KC * 512)
            wv = ring[slot][:, 0:KC * 512].rearrange("p (k n) -> p k n", n=512)
            for cc in range(4):
                for kc in range(KC):
                    S.op("pe", lambda e, wv=wv, kc=kc, cc=cc: e.matmul(
                        ps[6][:, cc:cc + 1], wv[:, kc, cc * 128:(cc + 1) * 128], cact[:, kc:kc + 1],
                        start=(kc == 0), stop=(kc == KC - 1)),
                        reads=(f"ring{slot}", "cact"), writes=("ps6",), signal=(kc == KC - 1))
            S.op("dve", lambda e, blk=blk: e.tensor_tensor(out=modT[:, blk * 4:(blk + 1) * 4], in0=ps[6][:, 0:4],
                                                          in1=badaT[:, blk * 4:(blk + 1) * 4], op=ALU.add),
                 reads=("ps6", "badaT"), writes=("modT",))

        def emit():
            consts = [(ident, ident_d, "ident"), (maskf, mask_d, "maskf"), (badaT, bada_d, "badaT"),
                      (nwT, nw_d, "nwT"), (cT, cT_d, "cT"), (keep, keep_d, "keep"),
                      (glanw, glanw_d, "glanw"), (hgnw, hgnw_d, "hgnw"), (wg, wg_d, "wg")]
            for t_, d_, key in consts:
                S.dma("sp", "c0", lambda e, t_=t_, d_=d_: e.dma_start(out=t_[:], in_=d_), writes=(key,))
            S.settle("c0", [c[2] for c in consts])
            S.dma("pool", "c1", lambda e: e.dma_start(out=wlr[:], in_=wlr_d), writes=("wlr",))
            S.op("dve", lambda e: e.memset(onesb[:], 1.0), writes=("onesb",))
            S.op("dve", lambda e: e.memset(epst[:], EPS), writes=("epst",))
            S.op("dve", lambda e: e.memset(onet[:], 1.0), writes=("onet",))
            S.op("dve", lambda e: e.tensor_copy(out=identb[:], in_=ident[:]), reads=("ident",), writes=("identb",))
            S.op("dve", lambda e: e.tensor_copy(out=maskb[:], in_=maskf[:]), reads=("maskf",), writes=("maskb",))
            S.op("dve", lambda e: e.memset(lrT[:], 1.0), writes=("lrT",))
            S.op("dve", lambda e: e.memset(Sstate, 0.0), writes=("Sstate",))
            S.op("act", lambda e: e.activation(out=cact[:], in_=cT[:], func=AF.Silu), reads=("cT",), writes=("cact",))
            phase1(0, False, part="a")
            for _ in range(8):
                mod_block()
            S.op("dve", lambda e: e.scalar_tensor_tensor(out=AB[:, 0:16], in0=modT[:, 16:32], scalar=1.0, in1=nwT[:, 0:16],
                                                         op0=ALU.add, op1=ALU.mult),
                 reads=("modT", "nwT"), writes=("AB",))
            S.op("dve", lambda e: e.tensor_copy(out=AB[:, 16:32], in_=modT[:, 0:16]), reads=("modT", "AB"), writes=("AB",))
            stage("p0", [("modT", modT[:]), ("AB", AB[:])])

            for slot in range(NSLOT):
                main = (slot == NSLOT - 1)
                phase1(slot, main, part=("b" if slot == 0 else "ab"))
                if slot == 0:
                    stage("p1", [("hT", av(0, KC * T, BF16)), ("rstd", rstd), ("xT", xTf)])
                S.barrier()
                lr_proj()
                if slot == 0:
                    stage("lr", [("lrT", lrT[:])])
                for g in range(4):
                    unit(slot, main, True, g, 2)
                    if main:
                        finish(2, 2 * g, glanw)
                    for j in range(2):
                        unit(slot, main, False, 2 * g + j, 1)
                        if main:
                            finish(1, 8 + 2 * g + j, hgnw)
                        if not main:
                            mod_block()
                    if not main:
                        mod_block()
                S.barrier()
                if slot == 0:
                    stage("s0", [("Sstate", Sstate)])
            while mod_state["blk"] < 24:
                mod_block()
            stage("p2", [("oT", av(32 * K, KC * T, BF16)), ("Sstate", Sstate), ("modT", modT[:])])
            S.op("dve", lambda e: e.scalar_tensor_tensor(out=AB[:, 32:48], in0=modT[:, 64:80], scalar=1.0, in1=nwT[:, 16:32],
                                                         op0=ALU.add, op1=ALU.mult),
                 reads=("modT", "nwT", "AB"), writes=("AB",))
            S.op("dve", lambda e: e.tensor_copy(out=AB[:, 48:64], in_=modT[:, 48:64]), reads=("modT", "AB"), writes=("AB",))
            S.barrier()
            phase34()

        def phase1(slot, main, part="ab"):
            xT_all = tuple(f"xT{t}{c}" for t in range(NT) for c in "ad")
            if "a" in part:
                phase1a(slot, main, xT_all)
            if "b" in part:
                phase1b(xT_all)

        def phase1a(slot, main, xT_all):
            for t in range(NT):
                sl = t % 2
                S.dma("sp", f"xs{sl}", lambda e, t=t, sl=sl: e.dma_start(out=xst[sl], in_=x_d[slot, t * 128:(t + 1) * 128, :]),
                      writes=(f"xst{sl}",))
                for g4 in range(4):
                    bank = 4 + (g4 % 2)
                    for j in range(4):
                        kc = g4 * 4 + j
                        S.op("pe", lambda e, sl=sl, kc=kc, j=j, bank=bank: e.transpose(
                            ps[bank][:, j * 128:(j + 1) * 128], xst[sl][:, kc * 128:(kc + 1) * 128], ident[:]),
                            reads=(f"xst{sl}", "ident"), writes=(f"ps{bank}",), signal=(j == 3))
                    dst = xT[:, g4 * 4:(g4 + 1) * 4, t * 128:(t + 1) * 128]
                    src = ps[bank][:].rearrange("p (a b) -> p a b", b=128)
                    if g4 % 2 == 0:
                        S.op("act", lambda e, dst=dst, src=src: e.activation(out=dst, in_=src, func=AF.Copy),
                             reads=(f"ps{bank}",), writes=(f"xT{t}a",))
                    else:
                        S.op("dve", lambda e, dst=dst, src=src: e.tensor_copy(out=dst, in_=src),
                             reads=(f"ps{bank}",), writes=(f"xT{t}d",))
            xT_all = tuple(f"xT{t}{c}" for t in range(NT) for c in "ad")
            if main:
                for q in range(4):
                    S.dma("sp", "xscrw", lambda e, q=q: e.dma_start(out=xscr.ap()[:, q * 4096:(q + 1) * 4096],
                                                                   in_=xTf[:, q * 4096:(q + 1) * 4096]),
                          reads=xT_all, writes=("xscr",))
                S.settle("xscrw", ["xscr"])
            for kc in range(KC):
                sl = kc % 2
                S.op("act", lambda e, kc=kc, sl=sl: e.activation(out=xsq[sl], in_=xT[:, kc, :], func=AF.Square),
                     reads=xT_all, writes=(f"xsq{sl}",))
                for tb in range(2):
                    S.op("pe", lambda e, kc=kc, sl=sl, tb=tb: e.matmul(
                        ps[5 + tb][:, :], onesb[:], xsq[sl][:, tb * 512:(tb + 1) * 512],
                        start=(kc == 0), stop=(kc == KC - 1)),
                        reads=(f"xsq{sl}", "onesb"), writes=(f"ps{5 + tb}",), signal=True)
            for tb in range(2):
                rstd_from(ps[5 + tb][:, :], rstd[:, tb * 512:(tb + 1) * 512], float(D), (f"ps{5 + tb}",), f"rstd{tb}")

        def phase1b(xT_all):
            for kc in range(KC):
                sl = kc % 2
                S.op("dve", lambda e, kc=kc, sl=sl: e.scalar_tensor_tensor(
                    out=tmpf[sl], in0=xT[:, kc, :], scalar=AB[:, kc:kc + 1], in1=rstd, op0=ALU.mult, op1=ALU.mult),
                    reads=xT_all + ("AB", "rstd0", "rstd1"), writes=(f"tmpf{sl}",))
                S.op("act", lambda e, kc=kc, sl=sl: e.activation(
                    out=hT[:, kc, :], in_=tmpf[sl], func=AF.Identity, bias=AB[:, 16 + kc:17 + kc], scale=1.0),
                    reads=(f"tmpf{sl}", "AB"), writes=("hT",))

        def lr_proj():
            wlr3 = wlr[:].rearrange("p (k n) -> p k n", n=16)
            for tb in range(2):
                for kc in range(KC):
                    S.op("pe", lambda e, kc=kc, tb=tb: e.matmul(
                        ps[6][0:16, :], wlr3[:, kc, :], hT[:, kc, tb * 512:(tb + 1) * 512],
                        start=(kc == 0), stop=(kc == KC - 1)),
                        reads=("wlr", "hT"), writes=("ps6",), signal=(kc == KC - 1))
                S.op("act", lambda e, tb=tb: e.activation(out=lrT[0:16, tb * 512:(tb + 1) * 512], in_=ps[6][0:16, :],
                                                          func=AF.Copy), reads=("ps6",), writes=("lrT",))

        def proj_fm(slot, ncols_blk, c0, evac):
            wv = ring[slot][:, 0:KC * ncols_blk].rearrange("p (k n) -> p k n", n=ncols_blk)
            for tb in range(2):
                b = nextbank()
                for kc in range(KC):
                    S.op("pe", lambda e, wv=wv, kc=kc, tb=tb, b=b: e.matmul(
                        ps[b][:, :], wv[:, kc, c0:c0 + 128], hT[:, kc, tb * 512:(tb + 1) * 512],
                        start=(kc == 0), stop=(kc == KC - 1)),
                        reads=(f"ring{slot}", "hT"), writes=(f"ps{b}",), signal=(kc == KC - 1))
                evac(tb, b)

        def proj_tm(slot, ncols_blk, c0, n, evac):
            wv = ring[slot][:, 0:KC * ncols_blk].rearrange("p (k n) -> p k n", n=ncols_blk)
            for t in range(NT):
                b = nextbank()
                for kc in range(KC):
                    S.op("pe", lambda e, wv=wv, kc=kc, t=t, b=b: e.matmul(
                        ps[b][:, 0:n], hT[:, kc, t * 128:(t + 1) * 128], wv[:, kc, c0:c0 + n],
                        start=(kc == 0), stop=(kc == KC - 1)),
                        reads=(f"ring{slot}", "hT"), writes=(f"ps{b}",), signal=(kc == KC - 1))
                evac(t, b)

        def unit(slot_i, main, is_gla, idx, nv):
            def ust(nm, aps):
                if slot_i == 0 and is_gla and idx == 0:
                    stage(nm, aps)
                if slot_i == 0 and (not is_gla) and idx == 0:
                    stage("h" + nm[1:], aps)
            Dv = nv * 128
            scol = idx * 256 if is_gla else 1024 + idx * 128
            bs = (-1.0 / 16.0) if is_gla else 1.0
            vt3 = vtok[:, 0:NT * Dv].rearrange("p (t e) -> p t e", e=Dv)
            Sb3 = Sbf[:, 0:16 * Dv].rearrange("p (c e) -> p c e", e=Dv)
            if is_gla:
                if main:
                    slotA = wload(wglaa_d[idx], KC * 512)
                    slotB = wload(wglav_d[idx], KC * 256)
                    nA, kc0, nB, vc0 = 512, 128, 256, 0
                else:
                    slotA = wload(wglap_d[idx], KC * 384)
                    slotB = slotA
                    nA, kc0, nB, vc0 = 384, 0, 384, 128
            else:
                if main:
                    slotA = wload(whg_d[idx], KC * 512)
                    nA, fc0 = 512, 256
                else:
                    slotA = wload(whgp_d[idx], KC * 256)
                    nA, fc0 = 256, 0
            if is_gla:
                for half in range(2):
                    b = nextbank()
                    for j in range(4):
                        t = half * 4 + j
                        S.op("pe", lambda e, t=t, j=j, b=b: e.matmul(
                            ps[b][:, j * 128:(j + 1) * 128], lrT[:, t * 128:(t + 1) * 128],
                            wg[:, idx * 128:(idx + 1) * 128], start=True, stop=True),
                            reads=("lrT", "wg"), writes=(f"ps{b}",), signal=(j == 3))
                    dst = graw[:, half * 4:(half + 1) * 4, :]
                    src = ps[b][:].rearrange("p (a b) -> p a b", b=128)
                    S.op("act", lambda e, dst=dst, src=src: e.activation(out=dst, in_=src, func=AF.Exp, scale=-1.0),
                         reads=(f"ps{b}",), writes=("graw",))
                S.op("act", lambda e: e.activation(out=graw, in_=graw, func=AF.Ln, bias=onet[:], scale=1.0),
                     reads=("graw", "onet"), writes=("graw",))
                if True:
                    def ev_v(t, b):
                        S.op("act", lambda e, t=t, b=b: e.activation(out=vt3[:, t, :], in_=ps[b][:, 0:256], func=AF.Copy),
                             reads=(f"ps{b}",), writes=("vtok",))
                    proj_tm(slotB, nB, vc0, 256, ev_v)
                if main:
                    for s_ in range(nv):
                        c0 = (256 + s_ * 128) if is_gla else 128

                        def ev_g(tb, b, s_=s_):
                            sl_ = slice(tb * 512, (tb + 1) * 512)
                            S.op("act", lambda e, b=b, sl_=sl_: e.activation(out=gate[:, s_, sl_], in_=ps[b][:, :], func=AF.Silu),
                                 reads=(f"ps{b}",), writes=("gate",))
                        proj_fm(slotA, nA, c0, ev_g)
            else:
                S.dma("sp", "lbl", lambda e: e.dma_start(out=lbt[:, 0:256], in_=lbl_d[idx]), writes=("lbt",))
                S.op("dve", lambda e: e.tensor_tensor(out=lbt[:, 512:640], in0=lbt[:, 128:256], in1=lbt[:, 0:128],
                                                      op=ALU.subtract), reads=("lbt",), writes=("lbt2",))
                S.op("act", lambda e: e.activation(out=lbt[:, 512:640], in_=lbt[:, 512:640], func=AF.Exp),
                     reads=("lbt2",), writes=("lbt2",))
                S.op("dve", lambda e: e.tensor_scalar(out=lbt[:, 256:384], in0=lbt[:, 512:640], scalar1=1.0, scalar2=None,
                                                      op0=ALU.add), reads=("lbt2",), writes=("lb",))
                S.op("dve", lambda e: e.reciprocal(out=lbt[:, 256:384], in_=lbt[:, 256:384]), reads=("lb",), writes=("lb",))
                S.op("dve", lambda e: e.tensor_tensor(out=lbt[:, 384:512], in0=lbt[:, 512:640], in1=lbt[:, 256:384],
                                                      op=ALU.mult), reads=("lb", "lbt2"), writes=("oml",))

                def ev_fi(t, b):
                    S.op("act", lambda e, t=t, b=b: e.activation(out=graw[:, t, :], in_=ps[b][:, 0:128], func=AF.Exp,
                                                                 scale=-1.0), reads=(f"ps{b}",), writes=("graw",))
                    S.op("act", lambda e, t=t, b=b: e.activation(out=vt3[:, t, :], in_=ps[b][:, 128:256], func=AF.Copy),
                         reads=(f"ps{b}",), writes=("vtok",))
                proj_tm(slotA, nA, fc0, 256, ev_fi)
                if main:
                    def ev_q0(tb, b):
                        sl_ = slice(tb * 512, (tb + 1) * 512)
                        S.op("act", lambda e, b=b, sl_=sl_: e.activation(out=qf[:, sl_], in_=ps[b][:, :], func=AF.Silu),
                             reads=(f"ps{b}",), writes=("qf",))
                    proj_fm(slotA, nA, 0, ev_q0)
                if main:
                    for s_ in range(nv):
                        c0 = (256 + s_ * 128) if is_gla else 128

                        def ev_g(tb, b, s_=s_):
                            sl_ = slice(tb * 512, (tb + 1) * 512)
                            S.op("act", lambda e, b=b, sl_=sl_: e.activation(out=gate[:, s_, sl_], in_=ps[b][:, :], func=AF.Silu),
                                 reads=(f"ps{b}",), writes=("gate",))
                        proj_fm(slotA, nA, c0, ev_g)
                S.op("dve", lambda e: e.tensor_scalar(out=graw, in0=graw, scalar1=1.0, scalar2=None, op0=ALU.add),
                     reads=("graw",), writes=("graw",))
                S.op("dve", lambda e: e.reciprocal(out=graw, in_=graw), reads=("graw",), writes=("graw",))
                S.op("dve", lambda e: e.tensor_tensor(out=graw, in0=graw,
                                                      in1=lbt[:, 384:512].unsqueeze(1).to_broadcast([128, NT, 128]),
                                                      op=ALU.mult), reads=("graw", "oml"), writes=("graw",))
                S.op("dve", lambda e: e.tensor_tensor(out=graw, in0=graw,
                                                      in1=lbt[:, 256:384].unsqueeze(1).to_broadcast([128, NT, 128]),
                                                      op=ALU.add), reads=("graw", "lb"), writes=("graw",))
                S.op("act", lambda e: e.activation(out=graw, in_=graw, func=AF.Ln), reads=("graw",), writes=("graw",))
            ust("ua", [("graw", av(TB0, NT * 128, F32))])
            ghi = ghl[:, 0:NT * 128].rearrange("p (t d) -> p t d", d=128)
            glo = ghl[:, NT * 128:2 * NT * 128].rearrange("p (t d) -> p t d", d=128)
            S.op("dve", lambda e: e.tensor_copy(out=ghi, in_=graw), reads=("graw",), writes=("ghi",))
            S.op("dve", lambda e: e.tensor_tensor(out=glo, in0=graw, in1=ghi, op=ALU.subtract), reads=("graw", "ghi"), writes=("glo",))
            for half in range(2):
                for j in range(4):
                    t = half * 4 + j
                    S.op("pe", lambda e, t=t, j=j, half=half: e.matmul(
                        ps[4 + half][:, j * 128:(j + 1) * 128], ghi[:, t, :], maskb[:], start=True, stop=False),
                        reads=("ghi", "maskb"), writes=(f"ps{4 + half}",), signal=False)
                    S.op("pe", lambda e, t=t, j=j, half=half: e.matmul(
                        ps[4 + half][:, j * 128:(j + 1) * 128], glo[:, t, :], maskb[:], start=False, stop=True),
                        reads=("glo", "maskb"), writes=(f"ps{4 + half}",), signal=(j == 3))
                sl_ = slice(half * 512, (half + 1) * 512)
                if main:
                    S.op("act", lambda e, half=half, sl_=sl_: e.activation(out=eb[:, sl_], in_=ps[4 + half][:, :], func=AF.Exp,
                                                                           scale=bs), reads=(f"ps{4 + half}",), writes=("eb",))
                S.op("act", lambda e, half=half, sl_=sl_: e.activation(out=enb[:, sl_], in_=ps[4 + half][:, :], func=AF.Exp,
                                                                       scale=-bs), reads=(f"ps{4 + half}",), writes=("enb",))
            ust("ub", [("enb", enb), ("small", small)])
            S.op("dve", lambda e: e.reciprocal(out=Ecol, in_=enb.rearrange("p (c j) -> p c j", j=64)[:, :, 63]),
                 reads=("enb",), writes=("Ecol",))
            if not is_gla:
                for half in range(2):
                    b = nextbank()
                    for j in range(4):
                        t = half * 4 + j
                        S.op("pe", lambda e, t=t, j=j, b=b: e.matmul(
                            ps[b][:, j * 128:(j + 1) * 128], ghi[:, t, :], identb[:], start=True, stop=False),
                            reads=("ghi", "identb"), writes=(f"ps{b}",), signal=False)
                        S.op("pe", lambda e, t=t, j=j, b=b: e.matmul(
                            ps[b][:, j * 128:(j + 1) * 128], glo[:, t, :], identb[:], start=False, stop=True),
                            reads=("glo", "identb"), writes=(f"ps{b}",), signal=(j == 3))
                    sl_ = slice(half * 512, (half + 1) * 512)
                    S.op("act", lambda e, b=b, sl_=sl_: e.activation(out=khf[:, sl_], in_=ps[b][:, :], func=AF.Exp),
                         reads=(f"ps{b}",), writes=("khf",))
                S.op("dve", lambda e: e.tensor_scalar(out=khf, in0=khf, scalar1=-1.0, scalar2=1.0, op0=ALU.mult, op1=ALU.add),
                     reads=("khf",), writes=("khf",))
                S.op("dve", lambda e: e.tensor_tensor(out=khf, in0=khf, in1=enb, op=ALU.mult),
                     reads=("khf", "enb"), writes=("khf",))
            qscale = 128.0 ** -0.5 if is_gla else 1.0
            if is_gla:
                if main:
                    def ev_q(tb, b):
                        sl_ = slice(tb * 512, (tb + 1) * 512)
                        S.op("dve", lambda e, b=b, sl_=sl_: e.scalar_tensor_tensor(
                            out=qt[:, sl_], in0=ps[b][:, :], scalar=qscale, in1=eb[:, sl_], op0=ALU.mult, op1=ALU.mult),
                            reads=(f"ps{b}", "eb"), writes=("qt",))
                    proj_fm(slotA, nA, 0, ev_q)

                def ev_k(tb, b):
                    sl_ = slice(tb * 512, (tb + 1) * 512)
                    S.op("dve", lambda e, b=b, sl_=sl_: e.tensor_tensor(out=khf[:, sl_], in0=ps[b][:, :], in1=enb[:, sl_],
                                                                        op=ALU.mult), reads=(f"ps{b}", "enb"), writes=("khf",))
                proj_fm(slotA, nA, kc0, ev_k)
            elif main:
                S.op("dve", lambda e: e.tensor_tensor(out=qt, in0=qf, in1=eb, op=ALU.mult),
                     reads=("qf", "eb"), writes=("qt",))
            ust("uc", [("khf", khf), ("small", small)])
            ust("ud", [("vtok", vtok)])
            if main:
                S.op("dve", lambda e: e.tensor_copy(out=kh, in_=khf), reads=("khf",), writes=("kh",))
            S.op("dve", lambda e: e.tensor_tensor(out=kt.rearrange("p (c j) -> p c j", j=64),
                                                  in0=khf.rearrange("p (c j) -> p c j", j=64),
                                                  in1=Ecol.unsqueeze(2).to_broadcast([128, 16, 64]), op=ALU.mult),
                 reads=("khf", "Ecol"), writes=("kt",))
            for t in range(NT):
                S.op("pe", lambda e, t=t: e.transpose(pst[:, t * 128:(t + 1) * 128], kt[:, t * 128:(t + 1) * 128], identb[:]),
                     reads=("kt", "identb"), writes=("pst",), signal=(t == NT - 1))
            S.op("act", lambda e: e.activation(out=ktok, in_=pst[:].rearrange("p (t d) -> p t d", d=128), func=AF.Copy),
                 reads=("pst",), writes=("ktok",))
            ust("ue", [("ktok", av(TB0 + 26 * K, NT * 128, BF16)), ("kt", kt)])
            if main:
                for half in range(2):
                    for j in range(4):
                        t = half * 4 + j
                        S.op("pe", lambda e, t=t, j=j, half=half: e.matmul(
                            ps[4 + half][:, j * 128:(j + 1) * 128], kh[:, t * 128:(t + 1) * 128], qt[:, t * 128:(t + 1) * 128],
                            start=True, stop=True), reads=("kh", "qt"), writes=(f"ps{4 + half}",), signal=(j == 3))
                    S.op("dve", lambda e, half=half: e.tensor_tensor(
                        out=sT[:, half * 4:(half + 1) * 4, :], in0=ps[4 + half][:].rearrange("p (a b) -> p a b", b=128),
                        in1=maskf[:].unsqueeze(1).to_broadcast([128, 4, 128]), op=ALU.mult),
                        reads=(f"ps{4 + half}", "maskf"), writes=("sT",))
            S.op("dve", lambda e: e.tensor_scalar(out=Sf[0][:, 0:Dv], in0=Sstate[:, scol:scol + Dv],
                                                  scalar1=keep[:, slot_i:slot_i + 1], scalar2=None, op0=ALU.mult),
                 reads=("Sstate", "keep"), writes=("Sf0",))
            if main:
                S.op("dve", lambda e: e.tensor_copy(out=Sb3[:, 0, :], in_=Sf[0][:, 0:Dv]), reads=("Sf0",), writes=("Sbf",))
            cpb = 512 // Dv
            for c0_ in range(0, NCH, 2 * cpb):
                bx, by = nextbank(), nextbank()
                loc = {}
                for j in range(2 * cpb):
                    c = c0_ + j
                    t, r0 = c // 2, (c % 2) * 64
                    bb = bx if c % 2 == 0 else by
                    col = (j // 2) * Dv
                    loc[c] = (bb, col)
                    S.op("pe", lambda e, t=t, r0=r0, bb=bb, col=col: e.matmul(
                        ps[bb][:, col:col + Dv], ktok[r0:r0 + 64, t, :], vt3[r0:r0 + 64, t, :], start=True, stop=True),
                        reads=("ktok", "vtok"), writes=(f"ps{bx}", f"ps{by}"), signal=(j == 2 * cpb - 1))
                for j in range(2 * cpb):
                    c = c0_ + j
                    bb, col = loc[c]
                    a_, b_ = Sf[c % 2], Sf[(c + 1) % 2]
                    if main and c < NCH - 1:
                        S.op("dve", lambda e, c=c, bb=bb, col=col, a_=a_: e.scalar_tensor_tensor(
                            out=Sb3[:, c + 1, :], in0=a_[:, 0:Dv], scalar=Ecol[:, c:c + 1], in1=ps[bb][:, col:col + Dv],
                            op0=ALU.mult, op1=ALU.add), reads=(f"Sf{c % 2}", "Ecol", f"ps{bb}"), writes=("Sbf",))
                    S.op("dve", lambda e, c=c, bb=bb, col=col, a_=a_, b_=b_: e.scalar_tensor_tensor(
                        out=b_[:, 0:Dv], in0=a_[:, 0:Dv], scalar=Ecol[:, c:c + 1], in1=ps[bb][:, col:col + Dv],
                        op0=ALU.mult, op1=ALU.add), reads=(f"Sf{c % 2}", "Ecol", f"ps{bb}"), writes=(f"Sf{(c + 1) % 2}",))
            S.op("dve", lambda e: e.tensor_copy(out=Sstate[:, scol:scol + Dv], in_=Sf[0][:, 0:Dv]),
                 reads=("Sf0", "Sstate"), writes=("Sstate",))
            ust("uf", [("Sstate", Sstate)])
            if not main:
                return
            for s_ in range(nv):
                bx, by = nextbank(), nextbank()
                for t in range(NT):
                    for cb in range(2):
                        r0 = cb * 64
                        c = 2 * t + cb
                        bb = bx if cb == 0 else by
                        o_ap = ps[bb][:, t * 64:(t + 1) * 64]
                        S.op("pe", lambda e, t=t, r0=r0, o_ap=o_ap, s_=s_: e.matmul(
                            o_ap, vt3[r0:r0 + 64, t, s_ * 128:(s_ + 1) * 128], sT[r0:r0 + 64, t, r0:r0 + 64],
                            start=True, stop=False), reads=("vtok", "sT"), writes=(f"ps{bx}", f"ps{by}"), signal=False)
                        S.op("pe", lambda e, t=t, r0=r0, o_ap=o_ap, c=c, s_=s_: e.matmul(
                            o_ap, Sb3[:, c, s_ * 128:(s_ + 1) * 128], qt[:, t * 128 + r0: t * 128 + r0 + 64],
                            start=False, stop=True), reads=("Sbf", "qt"), writes=(f"ps{bx}", f"ps{by}"),
                            signal=(t == NT - 1 and cb == 1))
                ol4 = oloc[:, s_, :].rearrange("p (t c j) -> p t c j", c=2, j=64)
                S.op("act", lambda e, bx=bx, ol4=ol4: e.activation(
                    out=ol4[:, :, 0, :], in_=ps[bx][:].rearrange("p (t j) -> p t j", j=64), func=AF.Copy),
                    reads=(f"ps{bx}",), writes=("oloc",))
                S.op("act", lambda e, by=by, ol4=ol4: e.activation(
                    out=ol4[:, :, 1, :], in_=ps[by][:].rearrange("p (t j) -> p t j", j=64), func=AF.Copy),
                    reads=(f"ps{by}", "oloc"), writes=("oloc",))
            stage("u%d" % (idx if is_gla else 4 + idx), [("oloc", av(121 * K, 2 * T, F32)), ("qt", qt), ("eb", eb),
                  ("enb", enb), ("khf", khf), ("graw", av(TB0, NT * 128, F32)), ("vtok", vtok), ("Sstate", Sstate),
                  ("small", small), ("lrT", lrT[:]), ("Sbf", Sbf)])

        def finish(nv, et0, nwt):
            for s_ in range(nv):
                S.op("act", lambda e, s_=s_: e.activation(out=osq, in_=oloc[:, s_, :], func=AF.Square),
                     reads=("oloc",), writes=("osq",))
                for tb in range(2):
                    S.op("pe", lambda e, tb=tb, s_=s_: e.matmul(
                        ps[5 + tb][:, :], onesb[:], osq[:, tb * 512:(tb + 1) * 512], start=(s_ == 0), stop=(s_ == nv - 1)),
                        reads=("osq", "onesb"), writes=(f"ps{5 + tb}",))
            for tb in range(2):
                rstd_from(ps[5 + tb][:, :], crstd[:, tb * 512:(tb + 1) * 512], float(nv * 128), (f"ps{5 + tb}",), f"crstd{tb}")
            for s_ in range(nv):
                S.op("dve", lambda e, s_=s_: e.scalar_tensor_tensor(
                    out=ctmp, in0=oloc[:, s_, :], scalar=nwt[:, s_:s_ + 1], in1=crstd, op0=ALU.mult, op1=ALU.mult),
                    reads=("oloc", "crstd0", "crstd1", "glanw", "hgnw"), writes=("ctmp",))
                S.op("dve", lambda e, s_=s_: e.tensor_tensor(out=oT[:, et0 + s_, :], in0=ctmp, in1=gate[:, s_, :],
                                                             op=ALU.mult), reads=("ctmp", "gate"), writes=("oT",))

        def phase34():
            x1T = av(64 * K, KC * T, F32).rearrange("p (k t) -> p k t", t=T)
            x1f = av(64 * K, KC * T, F32)
            sq2 = [av(128 * K + i * K, 512, BF16) for i in range(4)]
            for q in range(4):
                S.dma("sp", "xscrr", lambda e, q=q: e.dma_start(out=x1f[:, q * 4096:(q + 1) * 4096],
                                                               in_=xscr.ap()[:, q * 4096:(q + 1) * 4096]),
                      reads=("xscr",), writes=("x1T",))
            S.settle("xscrr", ["x1T"])
            sqn = {"n": 0}

            def resid_evac(b, dct, tb, gcol, ssbank, first, last):
                sl_ = slice(tb * 512, (tb + 1) * 512)
                S.op("dve", lambda e: e.scalar_tensor_tensor(
                    out=x1T[:, dct, sl_], in0=ps[b][:, :], scalar=modT[:, gcol:gcol + 1], in1=x1T[:, dct, sl_],
                    op0=ALU.mult, op1=ALU.add), reads=(f"ps{b}", "modT", "x1T"), writes=("x1T",))
                i = sqn["n"] % 4
                sqn["n"] += 1
                S.op("act", lambda e, i=i: e.activation(out=sq2[i], in_=x1T[:, dct, sl_], func=AF.Square),
                     reads=("x1T",), writes=(f"sq2{i}",))
                S.op("pe", lambda e, i=i: e.matmul(ps[ssbank][:, :], onesb[:], sq2[i], start=first, stop=last),
                     reads=(f"sq2{i}", "onesb"), writes=(f"ps{ssbank}",))

            for wb in range(8):
                slot = wload(wout_d[wb], KC * 256)
                wv = ring[slot][:, 0:KC * 256].rearrange("p (k n) -> p k n", n=256)
                for ct in range(2):
                    dct = wb * 2 + ct
                    for tb in range(2):
                        b = nextbank()
                        for kc in range(KC):
                            S.op("pe", lambda e, wv=wv, kc=kc, ct=ct, tb=tb, b=b: e.matmul(
                                ps[b][:, :], wv[:, kc, ct * 128:(ct + 1) * 128], oT[:, kc, tb * 512:(tb + 1) * 512],
                                start=(kc == 0), stop=(kc == KC - 1)),
                                reads=(f"ring{slot}", "oT"), writes=(f"ps{b}",), signal=(kc == KC - 1))
                        resid_evac(b, dct, tb, 32 + dct, 5 + tb, dct == 0, dct == KC - 1)
            stage("p3", [("x1T", x1f)])

            hid = av(0, HKC * 512, BF16).rearrange("p (k t) -> p k t", t=512)
            h2T = av(44 * K, KC * 512, BF16).rearrange("p (k t) -> p k t", t=512)
            rstd2 = av(60 * K, 512, F32)
            tmp2 = av(62 * K, 512, F32)
            sa = [av(132 * K + i * 2 * K, 512, F32) for i in range(2)]
            ost = [av(136 * K + i * 4 * K, 1024, F32) for i in range(2)]
            for tb in range(2):
                sl_ = slice(tb * 512, (tb + 1) * 512)
                rstd_from(ps[5 + tb][:, :], rstd2, float(D), (f"ps{5 + tb}",), "rstd2")
                for kc in range(KC):
                    S.op("dve", lambda e, kc=kc, sl_=sl_: e.scalar_tensor_tensor(
                        out=tmp2, in0=x1T[:, kc, sl_], scalar=AB[:, 32 + kc:33 + kc], in1=rstd2, op0=ALU.mult, op1=ALU.mult),
                        reads=("x1T", "AB", "rstd2"), writes=("tmp2",))
                    S.op("act", lambda e, kc=kc: e.activation(out=h2T[:, kc, :], in_=tmp2, func=AF.Identity,
                                                              bias=AB[:, 48 + kc:49 + kc], scale=1.0),
                         reads=("tmp2", "AB"), writes=("h2T",))
                for fb in range(22):
                    slot = wload(wf1_d[fb], KC * 512)
                    wv = ring[slot][:, 0:KC * 512].rearrange("p (k n) -> p k n", n=512)
                    for hh in range(2):
                        ht = fb * 2 + hh
                        ba, bu = nextbank(), nextbank()
                        for (bb, c0) in ((ba, hh * 128), (bu, 256 + hh * 128)):
                            for kc in range(KC):
                                S.op("pe", lambda e, wv=wv, kc=kc, bb=bb, c0=c0: e.matmul(
                                    ps[bb][:, :], wv[:, kc, c0:c0 + 128], h2T[:, kc, :], start=(kc == 0), stop=(kc == KC - 1)),
                                    reads=(f"ring{slot}", "h2T"), writes=(f"ps{bb}",), signal=(kc == KC - 1))
                        i = ht % 2
                        S.op("act", lambda e, ba=ba, i=i: e.activation(out=sa[i], in_=ps[ba][:, :], func=AF.Silu),
                             reads=(f"ps{ba}",), writes=(f"sa{i}",))
                        S.op("dve", lambda e, bu=bu, i=i, ht=ht: e.tensor_tensor(out=hid[:, ht, :], in0=ps[bu][:, :], in1=sa[i],
                                                                                op=ALU.mult), reads=(f"ps{bu}", f"sa{i}"), writes=("hid",))
                for dct in range(KC):
                    slot = wload(wf2_d[dct], HKC * 128)
                    wv = ring[slot][:, 0:HKC * 128].rearrange("p (k n) -> p k n", n=128)
                    b = nextbank()
                    for kc in range(HKC):
                        S.op("pe", lambda e, wv=wv, kc=kc, b=b: e.matmul(
                            ps[b][:, :], wv[:, kc, :], hid[:, kc, :], start=(kc == 0), stop=(kc == HKC - 1)),
                            reads=(f"ring{slot}", "hid"), writes=(f"ps{b}",), signal=(kc == HKC - 1))
                    resid_evac(b, dct, tb, 80 + dct, 4, dct == 0, dct == KC - 1)
                rstd_from(ps[4][:, :], rstd2, float(D), ("ps4",), "rstd2")
                for kc in range(KC):
                    S.op("dve", lambda e, kc=kc, sl_=sl_: e.scalar_tensor_tensor(
                        out=x1T[:, kc, sl_], in0=x1T[:, kc, sl_], scalar=nwT[:, 32 + kc:33 + kc], in1=rstd2,
                        op0=ALU.mult, op1=ALU.mult), reads=("x1T", "nwT", "rstd2"), writes=("x1T",))
                for tt in range(4):
                    t = tb * 4 + tt
                    for hf in range(2):
                        i = hf
                        for q in range(2):
                            b = nextbank()
                            for j in range(4):
                                kc = hf * 8 + q * 4 + j
                                S.op("pe", lambda e, kc=kc, j=j, b=b, t=t: e.transpose(
                                    ps[b][:, j * 128:(j + 1) * 128], x1T[:, kc, t * 128:(t + 1) * 128], ident[:]),
                                    reads=("x1T", "ident"), writes=(f"ps{b}",), signal=(j == 3))
                            if q == 0:
                                S.op("act", lambda e, b=b, i=i: e.activation(out=ost[i][:, 0:512], in_=ps[b][:, :], func=AF.Copy),
                                     reads=(f"ps{b}",), writes=(f"ost{i}",))
                            else:
                                S.op("dve", lambda e, b=b, i=i: e.tensor_copy(out=ost[i][:, 512:1024], in_=ps[b][:, :]),
                                     reads=(f"ps{b}",), writes=(f"ost{i}",))
                        S.dma("sp", f"ost{i}", lambda e, t=t, hf=hf, i=i: e.dma_start(
                            out=out_d[t * 128:(t + 1) * 128, hf * 1024:(hf + 1) * 1024], in_=ost[i]),
                            reads=(f"ost{i}",), writes=("outd",))

        try:
            emit()
        except _Stop:
            pass
        S.barrier()

        with nc.Block() as block:
            @block.tensor
            def _(eng):
                S.replay("pe", eng)

            @block.scalar
            def _(eng):
                S.replay("act", eng)

            @block.vector
            def _(eng):
                S.replay("dve", eng)

            @block.gpsimd
            def _(eng):
                S.replay("pool", eng)

            @block.sync
            def _(eng):
                S.replay("sp", eng)
    return nc


def _prep_inputs(x, c, w_ada, b_ada, norm_mix_w, w_in, w_gla_gate, b_gla_gate, gla_norm_w,
                 hg_lower_bound_logits, hg_norm_w, w_out, norm_ffn_w, w_ffn_in, w_ffn_out, final_norm_w):
    f = lambda a: np.ascontiguousarray(np.asarray(a, dtype=np.float32))
    x, c, w_ada, b_ada, w_in = f(x), f(c), f(w_ada)[0], f(b_ada)[0], f(w_in)[0]
    w_gla_gate, b_gla_gate = f(w_gla_gate)[0], f(b_gla_gate)[0]
    w_out, w_ffn_in, w_ffn_out = f(w_out)[0], f(w_ffn_in)[0], f(w_ffn_out)[0]
    lbl = f(hg_lower_bound_logits)

    def pk(v, n):
        return np.ascontiguousarray(v.reshape(n, 128).T)

    def blk(w, cols):
        kc = w.shape[0] // 128
        return np.ascontiguousarray(w[:, cols].reshape(kc, 128, len(cols)).transpose(1, 0, 2).reshape(128, -1))

    ar = np.arange
    shared = {}
    shared["b_adaT"] = pk(b_ada, 96)
    shared["nwT"] = np.ascontiguousarray(np.concatenate(
        [pk(f(norm_mix_w)[0], 16), pk(f(norm_ffn_w)[0], 16), pk(f(final_norm_w), 16)], axis=1))
    shared["w_gla_a"] = np.stack([blk(w_in, np.concatenate([ar(h * 128, (h + 1) * 128), 512 + ar(h * 128, (h + 1) * 128),
                                                            2048 + ar(h * 256, (h + 1) * 256)])) for h in range(4)])
    shared["w_gla_v"] = np.stack([blk(w_in, 1024 + ar(h * 256, (h + 1) * 256)) for h in range(4)])
    shared["w_hg_p"] = np.stack([blk(w_in, np.concatenate([4112 + ar(j * 128, (j + 1) * 128), 5136 + ar(j * 128, (j + 1) * 128)]))
                                 for j in range(8)])
    shared["w_gla_p"] = np.stack([blk(w_in, np.concatenate([512 + ar(h * 128, (h + 1) * 128), 1024 + ar(h * 256, (h + 1) * 256)]))
                                  for h in range(4)])
    shared["w_ada_b"] = np.stack([blk(w_ada, ar(q * 512, (q + 1) * 512)) for q in range(24)])
    hg = []
    for j in range(8):
        cols = np.concatenate([3088 + ar(j * 128, (j + 1) * 128), 6160 + ar(j * 128, (j + 1) * 128),
                               4112 + ar(j * 128, (j + 1) * 128), 5136 + ar(j * 128, (j + 1) * 128)])
        hg.append(blk(w_in, cols))
    shared["w_hg_u"] = np.stack(hg)
    shared["w_lr"] = blk(w_in, ar(3072, 3088))
    shared["wg_aug"] = np.ascontiguousarray(np.concatenate([w_gla_gate, b_gla_gate[None, :]], axis=0))
    shared["gla_nwT"] = pk(f(gla_norm_w)[0], 2)
    shared["hg_nwT"] = pk(f(hg_norm_w)[0], 1)
    lb = np.stack([np.concatenate([lbl[0, j * 128:(j + 1) * 128], lbl[1, j * 128:(j + 1) * 128]]) for j in range(8)])
    shared["lbl"] = np.ascontiguousarray(np.broadcast_to(lb[:, None, :], (8, 128, 256)))
    shared["w_out_b"] = np.stack([blk(w_out, ar(b * 256, (b + 1) * 256)) for b in range(8)])
    shared["w_f1"] = np.stack([blk(w_ffn_in, np.concatenate([ar(b * 256, (b + 1) * 256), FH + ar(b * 256, (b + 1) * 256)]))
                               for b in range(22)])
    shared["w_f2"] = np.stack([blk(w_ffn_out, ar(b * 128, (b + 1) * 128)) for b in range(16)])
    shared["ident"] = np.eye(128, dtype=np.float32)
    j = ar(128)[:, None]
    i = ar(128)[None, :]
    shared["maskT"] = ((j // 64 == i // 64) & (j <= i)).astype(np.float32)
    in_maps = []
    for r in range(NCORES):
        b, s = r // 4, r % 4
        m = dict(shared)
        segs = [max(j - (3 - s), 0) for j in range(3)] + [s]
        m["xs"] = np.ascontiguousarray(np.stack([x[b, q * T:(q + 1) * T, :] for q in segs]))
        m["cT"] = pk(c[b], 16)
        kp = np.ones((128, NSLOT), np.float32)
        kp[:, 3 - s] = 0.0
        m["keep"] = kp
        in_maps.append(m)
    return in_maps


_NC_CACHE = {}


def kernel(**inputs):
    in_maps = _prep_inputs(**inputs)
    if "nc" not in _NC_CACHE:
        _NC_CACHE["nc"] = build()
    res = run_bass_kernel_spmd(_NC_CACHE["nc"], in_maps, core_ids=list(range(NCORES)))
    out = np.empty((2, 4096, D), np.float32)
    for r in range(NCORES):
        b, s = r // 4, r % 4
        out[b, s * T:(s + 1) * T, :] = res.results[r]["out"]
    return out
```
